# Optimizing a Trainium2 kernel written in Bass

```python
import math
import jax
import jax.numpy as jnp
from jax import lax
import numpy as np

D_MODEL = 1024
BATCH = 16
SEQ = 2048
DEPTH = 4

GRID_W = 64
CTX_LEN = 256
EPS = 1e-6
N_MOD = 9
D_FF = 2816
FFN_RES = 0.5

SSD_HEADS = 16
SSD_HEAD_DIM = 64
SSD_INNER = SSD_HEADS * SSD_HEAD_DIM
SSD_GROUPS = 4
SSD_STATE = 128
SSD_CONV = 4
SSD_CHUNK = 128
SSD_XBC = SSD_INNER + 2 * SSD_GROUPS * SSD_STATE

MLA_HEADS = 8
MLA_Q_RANK = 384
MLA_KV_RANK = 256
MLA_NOPE = 128
MLA_ROPE = 64
MLA_V = 128
MLA_QK = MLA_NOPE + MLA_ROPE
ROPE_THETA = 10000.0
ATTN_BLOCK = 128

POOL_WINDOWS = (2, 4, 8, 16)
POOL_GROUP = 256
POOL_WIDTH = POOL_GROUP * len(POOL_WINDOWS)

CONV_WIDTH = 1024
CONV_K = 3

N_BRANCH = 4
IN_SPLITS = (SSD_INNER, SSD_XBC, 2 * SSD_HEADS, MLA_Q_RANK, MLA_KV_RANK, MLA_ROPE,
             POOL_WIDTH, 3 * CONV_WIDTH, N_BRANCH * D_MODEL)
IN_SPLIT_IDX = tuple(int(v) for v in np.cumsum(IN_SPLITS)[:-1])
D_IN = int(sum(IN_SPLITS))

kernel_name = 'hybrid_ssd_mla_pool_conv_prefix_dit'


def rms_norm(x, w):
    xf = x.astype(jnp.float32)
    y = xf * lax.rsqrt(jnp.mean(xf * xf, axis=-1, keepdims=True) + EPS)
    return (y * w.astype(jnp.float32)).astype(x.dtype)


def modulate(h, shift, scale):
    return h * (1 + scale) + shift


def swiglu(h, w_gate, w_up, w_down):
    return (jax.nn.silu(h @ w_gate) * (h @ w_up)) @ w_down


def depthwise_conv(x, w):
    k = w.shape[0]
    pad_l = k // 2
    return lax.conv_general_dilated(
        x, w[:, None, :].astype(x.dtype), window_strides=(1,),
        padding=[(pad_l, k - 1 - pad_l)],
        dimension_numbers=('NWC', 'WIO', 'NWC'),
        feature_group_count=x.shape[-1])


def seq_flip(t, reverse):
    return jnp.flip(t, axis=1) if reverse else t


def axial_rope_tables(n_tokens):
    rows = n_tokens // GRID_W
    row = jnp.repeat(jnp.arange(rows), GRID_W).astype(jnp.float32)
    col = jnp.tile(jnp.arange(GRID_W), rows).astype(jnp.float32)
    half = MLA_ROPE // 2
    inv = 1.0 / (ROPE_THETA ** (jnp.arange(0, half, 2, dtype=jnp.float32) / half))
    ar = row[:, None] * inv
    ac = col[:, None] * inv
    ang = jnp.concatenate([ar, ar, ac, ac], axis=-1)
    return jnp.cos(ang), jnp.sin(ang)


def apply_axial_rope(x, cos, sin):
    x1, x2, x3, x4 = jnp.split(x, 4, axis=-1)
    rot = jnp.concatenate([-x2, x1, -x4, x3], axis=-1)
    return (x.astype(jnp.float32) * cos + rot.astype(jnp.float32) * sin).astype(x.dtype)


def ssd_chunked(x, dt, a, b_in, c_in, init_state):
    bsz, n, h, p = x.shape
    g, ns = b_in.shape[-2:]
    r = h // g
    q = SSD_CHUNK
    nc = n // q
    xdt = (x.astype(jnp.float32) * dt[..., None]).reshape(bsz, nc, q, g, r, p)
    bc = b_in.astype(jnp.float32).reshape(bsz, nc, q, g, ns)
    cc = c_in.astype(jnp.float32).reshape(bsz, nc, q, g, ns)
    da = jnp.moveaxis((dt * a).reshape(bsz, nc, q, g, r), 2, -1)
    da_cs = jnp.cumsum(da, axis=-1)
    causal = jnp.tril(jnp.ones((q, q), dtype=bool))
    seg = da_cs[..., :, None] - da_cs[..., None, :]
    decay_in = jnp.exp(jnp.where(causal, seg, -jnp.inf))
    cb = jnp.einsum('bclgn,bcsgn->bcgls', cc, bc)
    y_diag = jnp.einsum('bcgrls,bcsgrp->bclgrp', cb[:, :, :, None] * decay_in, xdt)
    decay_to_end = jnp.exp(da_cs[..., -1:] - da_cs)
    chunk_states = jnp.einsum('bcsgn,bcgrs,bcsgrp->bcgrpn', bc, decay_to_end, xdt)

    def step(s, inp):
        chunk_decay, st = inp
        return s * jnp.exp(chunk_decay)[..., None, None] + st, s

    s0 = init_state.astype(jnp.float32).reshape(bsz, g, r, p, ns)
    s_final, s_prev = lax.scan(step, s0, (jnp.moveaxis(da_cs[..., -1], 1, 0),
                                          jnp.moveaxis(chunk_states, 1, 0)))
    s_prev = jnp.moveaxis(s_prev, 0, 1)
    y_off = jnp.einsum('bclgn,bcgrpn,bcgrl->bclgrp', cc, s_prev, jnp.exp(da_cs))
    return (y_diag + y_off).reshape(bsz, n, h, p), s_final.reshape(bsz, h, p, ns)


def ssd_inputs(xbc, dt_raw, conv_w, conv_b):
    xbc = jax.nn.silu(depthwise_conv(xbc, conv_w) + conv_b)
    bsz, n = xbc.shape[:2]
    xs, b_in, c_in = jnp.split(xbc, [SSD_INNER, SSD_INNER + SSD_GROUPS * SSD_STATE], axis=-1)
    return (xs.reshape(bsz, n, SSD_HEADS, SSD_HEAD_DIM),
            b_in.reshape(bsz, n, SSD_GROUPS, SSD_STATE),
            c_in.reshape(bsz, n, SSD_GROUPS, SSD_STATE),
            dt_raw.reshape(bsz, n, 2, SSD_HEADS).astype(jnp.float32))


def ssd_mixer(z_c, xbc_c, dt_c, z_l, xbc_l, dt_l, conv_w, conv_b, dt_bias, a_log, d_skip, norm_w, w_out):
    xs_c, b_c, c_c, dtr_c = ssd_inputs(xbc_c, dt_c, conv_w, conv_b)
    xs_l, b_l, c_l, dtr_l = ssd_inputs(xbc_l, dt_l, conv_w, conv_b)
    d_f = d_skip.astype(jnp.float32)[:, None]
    y_c = xs_c.astype(jnp.float32) * d_f
    y_l = xs_l.astype(jnp.float32) * d_f
    zero_state = jnp.zeros((xs_c.shape[0], SSD_HEADS, SSD_HEAD_DIM, SSD_STATE), jnp.float32)
    for direction in range(2):
        rev = direction == 1
        a = -jnp.exp(a_log[direction].astype(jnp.float32))
        bias = dt_bias[direction].astype(jnp.float32)
        dtc = jax.nn.softplus(dtr_c[:, :, direction] + bias)
        dtl = jax.nn.softplus(dtr_l[:, :, direction] + bias)
        yc, s_ctx = ssd_chunked(seq_flip(xs_c, rev), seq_flip(dtc, rev), a,
                                seq_flip(b_c, rev), seq_flip(c_c, rev), zero_state)
        yl, _ = ssd_chunked(seq_flip(xs_l, rev), seq_flip(dtl, rev), a,
                            seq_flip(b_l, rev), seq_flip(c_l, rev), s_ctx)
        y_c = y_c + seq_flip(yc, rev)
        y_l = y_l + seq_flip(yl, rev)
    yc = y_c.reshape(z_c.shape).astype(z_c.dtype) * jax.nn.silu(z_c)
    yl = y_l.reshape(z_l.shape).astype(z_l.dtype) * jax.nn.silu(z_l)
    return rms_norm(yc, norm_w) @ w_out, rms_norm(yl, norm_w) @ w_out


def mla_qkv(q_down, kv_down, k_rope, q_norm, w_uq, kv_norm, w_uk, w_uv, cos, sin):
    bsz, n = q_down.shape[:2]
    q = (rms_norm(q_down, q_norm) @ w_uq).reshape(bsz, n, MLA_HEADS, MLA_QK)
    c_kv = rms_norm(kv_down, kv_norm)
    k_nope = (c_kv @ w_uk).reshape(bsz, n, MLA_HEADS, MLA_NOPE)
    v = (c_kv @ w_uv).reshape(bsz, n, MLA_HEADS, MLA_V)
    q_nope, q_pe = jnp.split(q, [MLA_NOPE], axis=-1)
    if cos is not None:
        q_pe = apply_axial_rope(q_pe, cos[:, None, :], sin[:, None, :])
        k_rope = apply_axial_rope(k_rope, cos, sin)
    k_pe = jnp.broadcast_to(k_rope[:, :, None, :], (bsz, n, MLA_HEADS, MLA_ROPE))
    q = jnp.concatenate([q_nope, q_pe], axis=-1)
    k = jnp.concatenate([k_nope, k_pe], axis=-1)
    return q, k, v


def attend_blocks(q, k, v):
    bsz, lq, h, dk = q.shape
    nb = lq // ATTN_BLOCK
    qb = jnp.moveaxis(q.reshape(bsz, nb, ATTN_BLOCK, h, dk), 1, 0)
    scale = dk ** -0.5

    def one_block(q_blk):
        s = jnp.einsum('bqhd,bkhd->bhqk', q_blk, k).astype(jnp.float32) * scale
        p = jax.nn.softmax(s, axis=-1).astype(v.dtype)
        return jnp.einsum('bhqk,bkhd->bqhd', p, v)

    o = lax.map(one_block, qb)
    return jnp.moveaxis(o, 0, 1).reshape(bsz, lq, h, v.shape[-1])


def multiscale_pool(v):
    n = v.shape[1]
    t = jnp.arange(n)
    vf = v.astype(jnp.float32).reshape(v.shape[0], n, len(POOL_WINDOWS), POOL_GROUP)
    cs = jnp.concatenate([jnp.zeros_like(vf[:, :1]), jnp.cumsum(vf, axis=1)], axis=1)
    outs = []
    for gi, win in enumerate(POOL_WINDOWS):
        left = win // 2
        lo = jnp.clip(t - left, 0, n)
        hi = jnp.clip(t - left + win, 0, n)
        cnt = (hi - lo).astype(jnp.float32)[None, :, None]
        outs.append((cs[:, hi, gi] - cs[:, lo, gi]) / cnt - vf[:, :, gi])
    return jnp.stack(outs, axis=2).astype(v.dtype)


def pool_mixer(v, w_grp, scale, w_out):
    p = multiscale_pool(v)
    y = jnp.einsum('blgc,gcd->blgd', p, w_grp).reshape(v.shape) * scale
    return y @ w_out


def short_conv_mixer(u, conv_w, w_out):
    gate_b, gate_c, xin = jnp.split(u, 3, axis=-1)
    return (gate_b * depthwise_conv(gate_c * xin, conv_w)) @ w_out


def gated_merge(gate_logits, branches):
    g = jax.nn.sigmoid(gate_logits.astype(jnp.float32)).astype(gate_logits.dtype)
    g = g.reshape(gate_logits.shape[:-1] + (N_BRANCH, D_MODEL))
    out = g[..., 0, :] * branches[0]
    for k in range(1, N_BRANCH):
        out = out + g[..., k, :] * branches[k]
    return out


def token_mix(a_c, a_l, cos, sin, w_in, ssd_conv_w, ssd_conv_b, ssd_dt_bias, ssd_a_log, ssd_d,
              ssd_norm, ssd_w_out, mla_q_norm, mla_w_uq, mla_kv_norm, mla_w_uk, mla_w_uv,
              mla_w_out, pool_w, pool_scale, pool_w_out, sconv_w, sconv_w_out, w_o):
    (z_c, xbc_c, dt_c, qd_c, kvd_c, kr_c, pv_c, cv_c, g_c) = jnp.split(a_c @ w_in, IN_SPLIT_IDX, axis=-1)
    (z_l, xbc_l, dt_l, qd_l, kvd_l, kr_l, pv_l, cv_l, g_l) = jnp.split(a_l @ w_in, IN_SPLIT_IDX, axis=-1)
    bsz, n_ctx = a_c.shape[:2]
    n_lat = a_l.shape[1]

    ssd_c, ssd_l = ssd_mixer(z_c, xbc_c, dt_c, z_l, xbc_l, dt_l, ssd_conv_w, ssd_conv_b,
                             ssd_dt_bias, ssd_a_log, ssd_d, ssd_norm, ssd_w_out)

    q_c, k_c, v_c = mla_qkv(qd_c, kvd_c, kr_c, mla_q_norm, mla_w_uq, mla_kv_norm, mla_w_uk, mla_w_uv, None, None)
    q_l, k_l, v_l = mla_qkv(qd_l, kvd_l, kr_l, mla_q_norm, mla_w_uq, mla_kv_norm, mla_w_uk, mla_w_uv, cos, sin)
    att_c = attend_blocks(q_c, k_c, v_c)
    att_l = attend_blocks(q_l, jnp.concatenate([k_c, k_l], axis=1), jnp.concatenate([v_c, v_l], axis=1))
    mla_c = att_c.reshape(bsz, n_ctx, MLA_HEADS * MLA_V) @ mla_w_out
    mla_l = att_l.reshape(bsz, n_lat, MLA_HEADS * MLA_V) @ mla_w_out

    pool_c = pool_mixer(pv_c, pool_w, pool_scale, pool_w_out)
    pool_l = pool_mixer(pv_l, pool_w, pool_scale, pool_w_out)

    conv_c = short_conv_mixer(cv_c, sconv_w, sconv_w_out)
    conv_l = short_conv_mixer(cv_l, sconv_w, sconv_w_out)

    mix_c = gated_merge(g_c, (ssd_c, mla_c, pool_c, conv_c)) @ w_o
    mix_l = gated_merge(g_l, (ssd_l, mla_l, pool_l, conv_l)) @ w_o
    return mix_c, mix_l


def setup_inputs(seed: int = 0) -> dict:
    key = jax.random.key(seed)
    keys = iter(jax.random.split(key, 64))
    f32 = jnp.float32

    def normal(shape, scale):
        return jax.random.normal(next(keys), shape, f32) * scale

    def gain(shape):
        return 1.0 + 0.1 * jax.random.normal(next(keys), shape, f32)

    dt0 = jnp.exp(jax.random.uniform(next(keys), (DEPTH, 2, SSD_HEADS), f32,
                                     math.log(1e-3), math.log(1e-1)))
    dt_bias = dt0 + jnp.log(-jnp.expm1(-dt0))
    a_log = jnp.log(jax.random.uniform(next(keys), (DEPTH, 2, SSD_HEADS), f32, 1.0, 16.0))
    return {
        'x': normal((BATCH, SEQ, D_MODEL), 1.0),
        'c': normal((BATCH, D_MODEL), 1.0),
        'ctx': normal((BATCH, CTX_LEN, D_MODEL), 1.0),
        'c_ctx': normal((D_MODEL,), 1.0),
        'w_ada': normal((DEPTH, D_MODEL, N_MOD * D_MODEL), 0.5 * D_MODEL ** -0.5),
        'b_ada': normal((DEPTH, N_MOD * D_MODEL), 0.02),
        'ffn1_norm': gain((DEPTH, D_MODEL)),
        'ffn1_w_gate': normal((DEPTH, D_MODEL, D_FF), D_MODEL ** -0.5),
        'ffn1_w_up': normal((DEPTH, D_MODEL, D_FF), D_MODEL ** -0.5),
        'ffn1_w_down': normal((DEPTH, D_FF, D_MODEL), D_FF ** -0.5),
        'mix_norm': gain((DEPTH, D_MODEL)),
        'w_in': normal((DEPTH, D_MODEL, D_IN), D_MODEL ** -0.5),
        'ssd_conv_w': normal((DEPTH, SSD_CONV, SSD_XBC), SSD_CONV ** -0.5),
        'ssd_conv_b': normal((DEPTH, SSD_XBC), 0.02),
        'ssd_dt_bias': dt_bias,
        'ssd_a_log': a_log,
        'ssd_d': gain((DEPTH, SSD_HEADS)),
        'ssd_norm': gain((DEPTH, SSD_INNER)),
        'ssd_w_out': normal((DEPTH, SSD_INNER, D_MODEL), SSD_INNER ** -0.5),
        'mla_q_norm': gain((DEPTH, MLA_Q_RANK)),
        'mla_w_uq': normal((DEPTH, MLA_Q_RANK, MLA_HEADS * MLA_QK), MLA_Q_RANK ** -0.5),
        'mla_kv_norm': gain((DEPTH, MLA_KV_RANK)),
        'mla_w_uk': normal((DEPTH, MLA_KV_RANK, MLA_HEADS * MLA_NOPE), MLA_KV_RANK ** -0.5),
        'mla_w_uv': normal((DEPTH, MLA_KV_RANK, MLA_HEADS * MLA_V), MLA_KV_RANK ** -0.5),
        'mla_w_out': normal((DEPTH, MLA_HEADS * MLA_V, D_MODEL), (MLA_HEADS * MLA_V) ** -0.5),
        'pool_w': normal((DEPTH, len(POOL_WINDOWS), POOL_GROUP, POOL_GROUP), POOL_GROUP ** -0.5),
        'pool_scale': gain((DEPTH, POOL_WIDTH)),
        'pool_w_out': normal((DEPTH, POOL_WIDTH, D_MODEL), POOL_WIDTH ** -0.5),
        'sconv_w': normal((DEPTH, CONV_K, CONV_WIDTH), CONV_K ** -0.5),
        'sconv_w_out': normal((DEPTH, CONV_WIDTH, D_MODEL), CONV_WIDTH ** -0.5),
        'w_o': normal((DEPTH, D_MODEL, D_MODEL), D_MODEL ** -0.5),
        'ffn2_norm': gain((DEPTH, D_MODEL)),
        'ffn2_w_gate': normal((DEPTH, D_MODEL, D_FF), D_MODEL ** -0.5),
        'ffn2_w_up': normal((DEPTH, D_MODEL, D_FF), D_MODEL ** -0.5),
        'ffn2_w_down': normal((DEPTH, D_FF, D_MODEL), D_FF ** -0.5),
        'final_norm': gain((D_MODEL,)),
    }


def reference(x, c, ctx, c_ctx, w_ada, b_ada, ffn1_norm, ffn1_w_gate, ffn1_w_up, ffn1_w_down,
              mix_norm, w_in, ssd_conv_w, ssd_conv_b, ssd_dt_bias, ssd_a_log, ssd_d, ssd_norm,
              ssd_w_out, mla_q_norm, mla_w_uq, mla_kv_norm, mla_w_uk, mla_w_uv, mla_w_out,
              pool_w, pool_scale, pool_w_out, sconv_w, sconv_w_out, w_o, ffn2_norm, ffn2_w_gate,
              ffn2_w_up, ffn2_w_down, final_norm):
    cos, sin = axial_rope_tables(x.shape[1])
    h_l, h_c = x, ctx
    for i in range(DEPTH):
        m_l = jnp.split((jax.nn.silu(c) @ w_ada[i] + b_ada[i])[:, None, :], N_MOD, axis=-1)
        m_c = jnp.split(jax.nn.silu(c_ctx) @ w_ada[i] + b_ada[i], N_MOD, axis=-1)

        ffn1 = (ffn1_w_gate[i], ffn1_w_up[i], ffn1_w_down[i])
        h_c = h_c + FFN_RES * m_c[2] * swiglu(modulate(rms_norm(h_c, ffn1_norm[i]), m_c[0], m_c[1]), *ffn1)
        h_l = h_l + FFN_RES * m_l[2] * swiglu(modulate(rms_norm(h_l, ffn1_norm[i]), m_l[0], m_l[1]), *ffn1)

        a_c = modulate(rms_norm(h_c, mix_norm[i]), m_c[3], m_c[4])
        a_l = modulate(rms_norm(h_l, mix_norm[i]), m_l[3], m_l[4])
        o_c, o_l = token_mix(a_c, a_l, cos, sin, w_in[i], ssd_conv_w[i], ssd_conv_b[i], ssd_dt_bias[i],
                             ssd_a_log[i], ssd_d[i], ssd_norm[i], ssd_w_out[i], mla_q_norm[i],
                             mla_w_uq[i], mla_kv_norm[i], mla_w_uk[i], mla_w_uv[i], mla_w_out[i],
                             pool_w[i], pool_scale[i], pool_w_out[i], sconv_w[i], sconv_w_out[i], w_o[i])
        h_c = h_c + m_c[5] * o_c
        h_l = h_l + m_l[5] * o_l

        ffn2 = (ffn2_w_gate[i], ffn2_w_up[i], ffn2_w_down[i])
        h_c = h_c + FFN_RES * m_c[8] * swiglu(modulate(rms_norm(h_c, ffn2_norm[i]), m_c[6], m_c[7]), *ffn2)
        h_l = h_l + FFN_RES * m_l[8] * swiglu(modulate(rms_norm(h_l, ffn2_norm[i]), m_l[6], m_l[7]), *ffn2)
    return rms_norm(h_l, final_norm)
```

```python
import numpy as np
import concourse.bass as bass
import concourse.mybir as mybir
from concourse.bass_utils import run_bass_kernel_spmd
from contextlib import ExitStack

F32 = mybir.dt.float32
BF16 = mybir.dt.bfloat16
F32R = mybir.dt.float32r
AF = mybir.ActivationFunctionType
ALU = mybir.AluOpType

SAME_ENGINE_SYNC = True
N_DMA_SEMS = 12
MLA_PARTS = 3
FAST_RECIP = False
MLA_SKIP = ""
ROPE_Q = "dve"
QB = 7
QOFF = 128
ROPE_CUT = 0
NHEADS_DBG = 8

D = 1024
T = 2304
NCTX = 256
NLAT = 2048
DFF = 2816
EPS = 1e-6
TILES = [(0, 256), (256, 768), (768, 1280), (1280, 1792), (1792, 2304)]
NCH = 18
O_Z, O_XBC, O_DT, O_QD, O_KVD, O_KR, O_PV, O_CV, O_G = 0, 1024, 3072, 3104, 3488, 3744, 3808, 4832, 7904
D_IN = 12000


class Op:
    __slots__ = ("q", "fn", "reads", "writes", "is_dma", "deps", "signal", "sig_idx",
                 "dma_sem", "dma_target", "dma_prev", "idx")


class Sched:
    QUEUES = ("pe", "act", "dve", "pool", "sp")

    def __init__(self, nc, es):
        self.nc = nc
        self.esem = {q: es.enter_context(nc.semaphore("s_" + q)) for q in ("pe", "act", "dve", "pool")}
        self.dsem = {q: [es.enter_context(nc.semaphore("d_%s%d" % (q, i))) for i in range(N_DMA_SEMS)]
                     for q in ("sp", "pool", "act")}
        self.cnt = {q: 0 for q in self.esem}
        self.dcount = {q: [0] * N_DMA_SEMS for q in self.dsem}
        self.drr = {q: 0 for q in self.dsem}
        self.waited_e = {q: {e: 0 for e in self.esem} for q in self.QUEUES}
        self.waited_d = {q: {} for q in self.QUEUES}
        self.n_instr = 0
        self.phase_names = []
        self.reset()

    def reset(self):
        self.ops = []
        self.last_write = {}
        self.readers = {}

    def add(self, q, fn, reads=(), writes=(), is_dma=False):
        op = Op()
        op.q, op.fn, op.reads, op.writes, op.is_dma = q, fn, tuple(reads), tuple(writes), is_dma
        op.signal, op.sig_idx, op.dma_sem, op.dma_target, op.dma_prev = False, 0, None, 0, 0
        op.idx = len(self.ops)
        deps = set()
        lw, rd = self.last_write, self.readers
        for k in op.reads:
            w = lw.get(k)
            if w is not None:
                deps.add(w)
        for k in op.writes:
            w = lw.get(k)
            if w is not None:
                deps.add(w)
            r = rd.get(k)
            if r:
                deps.update(r.values())
        deps.discard(op.idx)
        newest = {}
        keep = []
        for d in deps:
            dop = self.ops[d]
            if dop.is_dma:
                keep.append(d)
            elif newest.get(dop.q, -1) < d:
                newest[dop.q] = d
        op.deps = sorted(keep + list(newest.values()))
        rkey = ("dma", op.idx) if is_dma else q
        for k in op.reads:
            rd.setdefault(k, {})[rkey] = op.idx
        for k in op.writes:
            lw[k] = op.idx
            rd[k] = {}
        self.ops.append(op)
        return op

    def _skip(self, dop, op):
        if dop.is_dma or op.is_dma or dop.q != op.q:
            return False
        return dop.q in ("pe", "pool") or not SAME_ENGINE_SYNC

    def emit(self, name="?"):
        nc, ops = self.nc, self.ops
        if not ops:
            return
        self.phase_names.append(name)
        for op in ops:
            for d in op.deps:
                dop = ops[d]
                if dop.is_dma or self._skip(dop, op):
                    continue
                dop.signal = True
        last = {}
        for op in ops:
            if not op.is_dma:
                last[op.q] = op
        for op in last.values():
            op.signal = True
        for op in ops:
            if op.is_dma:
                i = self.drr[op.q] % N_DMA_SEMS
                self.drr[op.q] += 1
                op.dma_sem = (op.q, i)
                op.dma_prev = self.dcount[op.q][i]
                self.dcount[op.q][i] += 16
                op.dma_target = self.dcount[op.q][i]
            elif op.signal:
                self.cnt[op.q] += 1
                op.sig_idx = self.cnt[op.q]
        per_q = {q: [] for q in self.QUEUES}
        for op in ops:
            per_q[op.q].append(op)
        self.n_instr += len(ops)
        esem, dsem = self.esem, self.dsem
        final_e = dict(self.cnt)
        final_d = {q: list(v) for q, v in self.dcount.items()}

        def make(qname):
            def body(eng):
                if TRACE is not None:
                    eng = _TraceEng(eng, qname, TRACE)
                we = self.waited_e[qname]
                wd = self.waited_d[qname]

                def wait_for(dop):
                    if dop.is_dma:
                        key = dop.dma_sem
                        if wd.get(key, 0) < dop.dma_target:
                            eng.wait_ge(dsem[key[0]][key[1]], dop.dma_target)
                            wd[key] = dop.dma_target
                    else:
                        if we[dop.q] < dop.sig_idx:
                            eng.wait_ge(esem[dop.q], dop.sig_idx)
                            we[dop.q] = dop.sig_idx

                for op in per_q[qname]:
                    for d in op.deps:
                        dop = ops[d]
                        if self._skip(dop, op):
                            continue
                        wait_for(dop)
                    if op.is_dma:
                        key = op.dma_sem
                        if op.dma_prev > 0 and wd.get(key, 0) < op.dma_prev:
                            eng.wait_ge(dsem[key[0]][key[1]], op.dma_prev)
                            wd[key] = op.dma_prev
                        op.fn(eng).then_inc(dsem[key[0]][key[1]], 16)
                    else:
                        ins = op.fn(eng)
                        if op.signal:
                            ins.then_inc(esem[qname], 1)
                if qname == "sp":
                    for e, v in final_e.items():
                        if we[e] < v:
                            eng.wait_ge(esem[e], v)
                            we[e] = v
                    for dq, lst in final_d.items():
                        for i, v in enumerate(lst):
                            if v > 0 and wd.get((dq, i), 0) < v:
                                eng.wait_ge(dsem[dq][i], v)
                                wd[(dq, i)] = v
            return body

        with nc.Block() as block:
            block.tensor(make("pe"))
            block.scalar(make("act"))
            block.vector(make("dve"))
            block.gpsimd(make("pool"))
            block.sync(make("sp"))
        self.reset()


TRACE = None


class _TraceIns:
    def __init__(self, ins, q, log):
        self.ins, self.q, self.log = ins, q, log

    def then_inc(self, sem, n):
        self.log.append((self.q, "inc", id(sem), n))
        return self.ins.then_inc(sem, n)


class _TraceEng:
    def __init__(self, eng, q, log):
        self._e, self._q, self._log = eng, q, log

    def wait_ge(self, sem, val):
        self._log.append((self._q, "wait", id(sem), val))
        return self._e.wait_ge(sem, val)

    def __getattr__(self, name):
        f = getattr(self._e, name)

        def wrap(*a, **k):
            self._log.append((self._q, "op", name, 0))
            return _TraceIns(f(*a, **k), self._q, self._log)
        return wrap


class G:
    pass


def cols_layout(depth):
    per = [("gam0", 8), ("gam1", 8), ("gam2", 8), ("bada", 72), ("scw", 64), ("scb", 16), ("ssdn", 8),
           ("qn", 3), ("kvn", 2), ("psc", 8), ("cvw", 24)]
    off, o = {}, 0
    for l in range(depth):
        for n, w in per:
            off[(n, l)] = o
            o += w
    off["fn"] = o
    o += 8
    return off, o


def build(depth=4, nb=2, stop_after=None, dbg_shape=None, dbg_dtype=None):
    nc = bass.Bass("TRN2", target_bir_lowering=False)
    g = G()
    g.nc, g.depth, g.nb = nc, depth, nb

    def din(name, shape, dt=F32):
        return nc.dram_tensor(name, list(shape), dt, kind="ExternalInput").ap()

    g.x = din("x", [nb, NLAT, D])
    g.ctx = din("ctx", [nb, NCTX, D])
    g.cT = din("cT", [128, 8, 3])
    g.w_ada = din("w_ada", [depth, D, 9 * D])
    wnames = {}
    for pre in ("ffn1", "ffn2"):
        wnames[pre + "_w_gate"] = [depth, D, DFF]
        wnames[pre + "_w_up"] = [depth, D, DFF]
        wnames[pre + "_w_down"] = [depth, DFF, D]
    wnames.update({"w_in": [depth, D, D_IN], "ssd_w_out": [depth, D, D], "mla_w_uq": [depth, 384, 1536],
                   "mla_w_uk": [depth, 256, 1024], "mla_w_uv": [depth, 256, 1024], "mla_w_out": [depth, D, D],
                   "pool_w": [depth, 4, 256, 256], "pool_w_out": [depth, D, D], "sconv_w_out": [depth, D, D],
                   "w_o": [depth, D, D]})
    g.w = {k: din(k, v) for k, v in wnames.items()}
    g.coff, ncols = cols_layout(depth)
    g.cols_d = din("cols", [128, ncols])
    g.rows_d = din("rows", [depth, 3, 32])
    g.consts_d = din("consts", [128, 6 * 128])
    g.rope_d = din("rope", [2, 64, T])
    g.rot_d = din("rotm", [64, 64])
    g.invc_d = din("invcnt", [4, T])
    g.out = nc.dram_tensor("out", [nb, NLAT, D], F32, kind="ExternalOutput").ap()
    g.hs = nc.dram_tensor("hs", [nb, 128, 8, T], F32, kind="Internal").ap()
    g.xbcT = nc.dram_tensor("xbcT", [16, 128, T], BF16, kind="Internal").ap()
    g.sbwd = nc.dram_tensor("sbwd", [NCH, 128, 1024], BF16, kind="Internal").ap()
    g.mg_d = nc.dram_tensor("mg_d", [8, 128, T], BF16, kind="Internal").ap()
    g.dbg = None
    if stop_after is not None:
        g.dbg = nc.dram_tensor("dbg", list(dbg_shape), dbg_dtype, kind="ExternalOutput").ap()

    with ExitStack() as es:
        S = Sched(nc, es)
        g.S = S

        g.uid = 0

        def sb(es_, name, shape, dt):
            g.uid += 1
            return es_.enter_context(nc.sbuf_tensor("%s_u%d" % (name, g.uid), list(shape), dt))
        g.sb = sb
        g.ps = [es.enter_context(nc.psum_tensor("ps%d" % i, [128, 512], F32)) for i in range(8)]
        g.cols = sb(es, "cols", [128, ncols], F32)
        g.consts = sb(es, "consts", [128, 6 * 128], F32)
        g.identb = sb(es, "identb", [128, 128], BF16)
        g.onesb = sb(es, "onesb", [128, 128], BF16)
        g.mod = sb(es, "mod", [128, depth, 72, 3], F32)
        g.Gm = sb(es, "Gmod", [128, depth, 3, 8, 3], F32)
        g.Gt = sb(es, "Gate", [128, depth, 3, 8, 3], F32)
        g.aT = sb(es, "aT", [128, 8, T], BF16)
        g.identf = g.consts[:, 0:128]
        g.onesf = g.consts[:, 128:256]
        g.Lf = g.consts[:, 256:384]
        g.Ub = g.consts[:, 384:512]
        g.SL = g.consts[:, 512:640]
        g.SU = g.consts[:, 640:768]

        phases = Phases(g)
        phases.run(stop_after)
    g.n_instr = S.n_instr
    g.phase_names = S.phase_names
    return nc, g


def col(g, name, l, j):
    o = g.coff[(name, l)] + j
    return g.cols[:, o:o + 1]


def tile_of(kt):
    t = kt * 128
    for ti, (t0, t1) in enumerate(TILES):
        if t0 <= t < t1:
            return ti


def cls_of(t0, bl):
    return 2 if t0 < NCTX else bl


class Phases:
    def __init__(self, g):
        self.g = g
        self.S = g.S

    def DMA(self, q, out, in_, r, w):
        return self.S.add(q, lambda e: e.dma_start(out=out, in_=in_), r, w, True)

    def MM(self, out, lhsT, rhs, st, sp, r, w):
        return self.S.add("pe", lambda e: e.matmul(out, lhsT=lhsT, rhs=rhs, start=st, stop=sp), r, w)

    def TR(self, out, in_, ident, r, w):
        return self.S.add("pe", lambda e: e.transpose(out=out, in_=in_, identity=ident), r, w)

    def ACT(self, out, in_, func, r, w, bias=None, scale=None, accum_out=None):
        kw = {}
        if bias is not None:
            kw["bias"] = bias
        if scale is not None:
            kw["scale"] = scale
        if accum_out is not None:
            kw["accum_out"] = accum_out
        return self.S.add("act", lambda e: e.activation(out=out, in_=in_, func=func, **kw), r, w)

    def TT(self, q, out, in0, in1, op, r, w):
        return self.S.add(q, lambda e: e.tensor_tensor(out=out, in0=in0, in1=in1, op=op), r, w)

    def TS(self, q, out, in0, s1, op0, r, w, s2=None, op1=None):
        if op1 is None:
            return self.S.add(q, lambda e: e.tensor_scalar(out=out, in0=in0, scalar1=s1, scalar2=None, op0=op0), r, w)
        return self.S.add(q, lambda e: e.tensor_scalar(out=out, in0=in0, scalar1=s1, scalar2=s2, op0=op0, op1=op1), r, w)

    def STT(self, out, in0, scalar, in1, op0, op1, r, w):
        return self.S.add("dve", lambda e: e.scalar_tensor_tensor(out=out, in0=in0, scalar=scalar, in1=in1,
                                                                  op0=op0, op1=op1), r, w)

    def CP(self, q, out, in_, r, w):
        if q == "act":
            return self.ACT(out, in_, AF.Copy, r, w)
        return self.S.add(q, lambda e: e.tensor_copy(out=out, in_=in_), r, w)

    def RECIP(self, out, in_, r, w):
        if FAST_RECIP:
            return self.S.add("dve", lambda e: e.reciprocal_approx_fast(out=out, in_=in_), r, w)
        return self.S.add("dve", lambda e: e.reciprocal(out=out, in_=in_), r, w)

    def MEMSET(self, q, ap, val, w):
        return self.S.add(q, lambda e: e.memset(ap, val), (), w)

    def wslab(self, dst, src2d, key, p=128):
        return self.DMA("pool", dst, src2d.rearrange("(k p) n -> p k n", p=p), [], [key])

    def done(self, name, stop_after, dump=None):
        self.S.emit("done.1")
        if stop_after == name:
            g = self.g
            if dump is not None:
                src, q = dump
                self.DMA(q, g.dbg, src, [], [])
                self.S.emit("done.2")
            return True
        return False

    def run(self, stop_after):
        g = self.g
        self.phase_setup()
        if self.done("setup", stop_after, (g.hs[0], "sp")):
            return
        self.phase_ada()
        if self.done("ada", stop_after, (g.mod[:, 0], "sp")):
            return
        for l in range(g.depth):
            for b in range(g.nb):
                tag = "" if (l == 0 and b == 0) else "_%d_%d" % (l, b)
                self.phase_norm(l, b, 0)
                if self.done("norm0" + tag, stop_after, (g.aT[:], "sp")):
                    return
                self.phase_ffn(l, b, 0)
                if self.done("ffn1" + tag, stop_after, (g.hs[b], "sp")):
                    return
                self.phase_norm(l, b, 1)
                if self.done("norm1" + tag, stop_after, (g.aT[:], "sp")):
                    return
                with ExitStack() as mes:
                    g.yX = g.sb(mes, "yX", [128, 8, T], BF16)
                    self.phase_sconv(l, b)
                    if self.done("sconv" + tag, stop_after, (g.yX[:], "sp")):
                        return
                    self.phase_branch_out(l, b, g.w["sconv_w_out"][l], 3, True)
                    if self.done("sconv_out" + tag, stop_after, (g.mg_d.rearrange("k p t -> p k t"), "sp")):
                        return
                    self.phase_pool(l, b)
                    if self.done("pool" + tag, stop_after, (g.yX[:], "sp")):
                        return
                    self.phase_branch_out(l, b, g.w["pool_w_out"][l], 2, False)
                    if self.done("pool_out" + tag, stop_after, (g.mg_d.rearrange("k p t -> p k t"), "sp")):
                        return
                    self.phase_mla(l, b)
                    if self.done("mla" + tag, stop_after, (g.yX[:], "sp")):
                        return
                    self.phase_branch_out(l, b, g.w["mla_w_out"][l], 1, False)
                    if self.done("mla_out" + tag, stop_after, (g.mg_d.rearrange("k p t -> p k t"), "sp")):
                        return
                    if self.phase_ssd(l, b, stop_after, tag):
                        return
                    self.phase_ssd_post(l, b)
                    if self.done("ssd" + tag, stop_after, (g.yX[:], "sp")):
                        return
                    self.phase_branch_out(l, b, g.w["ssd_w_out"][l], 0, False)
                    if self.done("ssd_out" + tag, stop_after, (g.mg_d.rearrange("k p t -> p k t"), "sp")):
                        return
                    self.phase_wo(l, b)
                    if self.done("wo" + tag, stop_after, (g.hs[b], "sp")):
                        return
                self.phase_norm(l, b, 2)
                self.phase_ffn(l, b, 2)
                if self.done("ffn2" + tag, stop_after, (g.hs[b], "sp")):
                    return
        self.phase_final()
        self.S.emit("run.1")

    def phase_setup(self):
        g = self.g
        self.DMA("sp", g.cols[:], g.cols_d, [], ["cols"])
        self.DMA("sp", g.consts[:], g.consts_d, [], ["consts"])
        self.CP("dve", g.identb[:], g.identf, ["consts"], ["identb"])
        self.CP("dve", g.onesb[:], g.onesf, ["consts"], ["onesb"])
        with ExitStack() as es:
            xin = [g.sb(es, "xin%d" % i, [128, D], F32) for i in range(2)]
            xo = [g.sb(es, "xo%d" % i, [128, 8, 128], F32) for i in range(2)]
            n = 0
            for b in range(g.nb):
                for tt in range(NCH):
                    i = n % 2
                    n += 1
                    src = g.ctx[b, tt * 128:(tt + 1) * 128, :] if tt < 2 else g.x[b, (tt - 2) * 128:(tt - 1) * 128, :]
                    self.DMA("sp", xin[i][:], src, [], ["xin%d" % i])
                    for k in range(8):
                        bk = i * 2 + k // 4
                        self.TR(g.ps[bk][:, (k % 4) * 128:(k % 4 + 1) * 128], xin[i][:, k * 128:(k + 1) * 128],
                                g.identf, ["xin%d" % i, "consts"], ["ps%d" % bk])
                    self.ACT(xo[i][:, 0:4, :], g.ps[i * 2][:].rearrange("p (a b) -> p a b", a=4), AF.Copy,
                             ["ps%d" % (i * 2)], [("xo", i, 0)])
                    self.CP("dve", xo[i][:, 4:8, :], g.ps[i * 2 + 1][:].rearrange("p (a b) -> p a b", a=4),
                            ["ps%d" % (i * 2 + 1)], [("xo", i, 1)])
                    self.DMA("pool", g.hs[b][:, :, tt * 128:(tt + 1) * 128], xo[i][:], [("xo", i, 0), ("xo", i, 1)],
                             [("hs", b, tt)])
            self.S.emit("setup.1")

    def phase_ada(self):
        g = self.g
        with ExitStack() as es:
            ct = g.sb(es, "ct", [128, 8, 3], F32)
            sc = g.sb(es, "sc", [128, 8, 3], F32)
            wa = [g.sb(es, "wa%d" % i, [128, 8, 512], F32) for i in range(2)]
            self.DMA("sp", ct[:], g.cT, [], ["ct"])
            self.ACT(sc[:], ct[:], AF.Silu, ["ct"], ["sc"])
            n = 0
            for l in range(g.depth):
                for j in range(18):
                    i = n % 2
                    n += 1
                    self.DMA("sp", wa[i][:], g.w_ada[l][:, j * 512:(j + 1) * 512].rearrange("(k p) n -> p k n", p=128),
                             [], ["wa%d" % i])
                    for oc in range(4):
                        for k in range(8):
                            self.MM(g.ps[i][:, oc * 3:(oc + 1) * 3], wa[i][:, k, oc * 128:(oc + 1) * 128], sc[:, k, :],
                                    k == 0, k == 7, ["wa%d" % i, "sc"], ["ps%d" % i])
                    o = g.coff[("bada", l)] + j * 4
                    self.TT("dve", g.mod[:, l, j * 4:(j + 1) * 4, :],
                            g.ps[i][:, 0:12].rearrange("p (a b) -> p a b", a=4),
                            g.cols[:, o:o + 4].unsqueeze(2).broadcast_to([128, 4, 3]), ALU.add,
                            ["ps%d" % i, "cols"], [("mod", l, j)])
                for n_ in range(3):
                    o = g.coff[("gam%d" % n_, l)]
                    rk = [("mod", l, j) for j in range(18)]
                    self.STT(g.Gm[:, l, n_], g.mod[:, l, (3 * n_ + 1) * 8:(3 * n_ + 2) * 8, :], 1.0,
                             g.cols[:, o:o + 8].unsqueeze(2).broadcast_to([128, 8, 3]), ALU.add, ALU.mult,
                             rk + ["cols"], [("Gm", l, n_)])
                    self.TS("dve", g.Gt[:, l, n_], g.mod[:, l, (3 * n_ + 2) * 8:(3 * n_ + 3) * 8, :],
                            1.0 if n_ == 1 else 0.5, ALU.mult, rk, [("Gt", l, n_)])
            self.S.emit("ada.1")

    def phase_norm(self, l, b, n):
        g = self.g
        with ExitStack() as es:
            ht = g.sb(es, "ht", [128, 8, T], F32)
            sq = [g.sb(es, "sq%d" % i, [128, 8, 512], BF16) for i in range(2)]
            rs = [g.sb(es, "rs%d" % i, [128, 512], F32) for i in range(5)]
            tmp = [g.sb(es, "tmp%d" % i, [128, 8, 512], F32) for i in range(2)]
            for ti, (t0, t1) in enumerate(TILES):
                w = t1 - t0
                i = ti % 2
                self.DMA("sp", ht[:, :, t0:t1], g.hs[b][:, :, t0:t1], [], [("ht", ti)])
                self.ACT(sq[i][:, :, :w], ht[:, :, t0:t1], AF.Square, [("ht", ti)], ["sq%d" % i])
                for k in range(8):
                    self.MM(g.ps[i][:, :w], g.onesb[:], sq[i][:, k, :w], k == 0, k == 7, ["sq%d" % i], ["ps%d" % i])
                self.ACT(rs[ti][:, :w], g.ps[i][:, :w], AF.Sqrt, ["ps%d" % i], ["rs%d" % ti], bias=EPS, scale=1.0 / D)
                self.RECIP(rs[ti][:, :w], rs[ti][:, :w], ["rs%d" % ti], ["rs%d" % ti])
            for ti, (t0, t1) in enumerate(TILES):
                w = t1 - t0
                i = ti % 2
                cl = cls_of(t0, b)
                self.TT("dve", tmp[i][:, :, :w], ht[:, :, t0:t1], rs[ti][:, :w].unsqueeze(1).broadcast_to([128, 8, w]),
                        ALU.mult, [("ht", ti), "rs%d" % ti], ["tmp%d" % i])
                for k in range(8):
                    if k % 3 == 2:
                        self.TS("pool", g.aT[:, k, t0:t1], tmp[i][:, k, :w], g.Gm[:, l, n, k, cl:cl + 1], ALU.mult,
                                ["tmp%d" % i], [("aT", k, ti)], s2=g.mod[:, l, 3 * n * 8 + k, cl:cl + 1], op1=ALU.add)
                    else:
                        self.ACT(g.aT[:, k, t0:t1], tmp[i][:, k, :w], AF.Identity, ["tmp%d" % i], [("aT", k, ti)],
                                 scale=g.Gm[:, l, n, k, cl:cl + 1], bias=g.mod[:, l, 3 * n * 8 + k, cl:cl + 1])
            self.S.emit("norm.1")

    def phase_ffn(self, l, b, n):
        g = self.g
        pre = "ffn1" if n == 0 else "ffn2"
        wg_d, wu_d, wd_d = g.w[pre + "_w_gate"][l], g.w[pre + "_w_up"][l], g.w[pre + "_w_down"][l]
        aTk = [("aT", k, ti) for k in range(8) for ti in range(5)]
        with ExitStack() as es:
            hT = g.sb(es, "hT", [128, 22, T], BF16)
            with ExitStack() as es2:
                wg = [g.sb(es2, "wg%d" % i, [128, 8, 256], BF16) for i in range(2)]
                wu = [g.sb(es2, "wu%d" % i, [128, 8, 256], BF16) for i in range(2)]
                sg = [g.sb(es2, "sg%d" % i, [128, 512], F32) for i in range(2)]
                n2 = 0
                for fs in range(11):
                    i = fs % 2
                    self.wslab(wg[i][:], wg_d[:, fs * 256:(fs + 1) * 256], "wg%d" % i)
                    self.wslab(wu[i][:], wu_d[:, fs * 256:(fs + 1) * 256], "wu%d" % i)
                    for fo in range(2):
                        f = fs * 2 + fo
                        for ti, (t0, t1) in enumerate(TILES):
                            w = t1 - t0
                            j = n2 % 2
                            n2 += 1
                            pg, pu = g.ps[j * 2], g.ps[j * 2 + 1]
                            for k in range(8):
                                self.MM(pg[:, :w], wg[i][:, k, fo * 128:(fo + 1) * 128], g.aT[:, k, t0:t1], k == 0, k == 7,
                                        ["wg%d" % i], ["ps%d" % (j * 2)])
                            for k in range(8):
                                self.MM(pu[:, :w], wu[i][:, k, fo * 128:(fo + 1) * 128], g.aT[:, k, t0:t1], k == 0, k == 7,
                                        ["wu%d" % i], ["ps%d" % (j * 2 + 1)])
                            self.ACT(sg[j][:, :w], pg[:, :w], AF.Silu, ["ps%d" % (j * 2)], ["sg%d" % j])
                            self.TT("dve", hT[:, f, t0:t1], sg[j][:, :w], pu[:, :w], ALU.mult,
                                    ["sg%d" % j, "ps%d" % (j * 2 + 1)], [("hT", f, ti)])
                self.S.emit("ffn.1")
            with ExitStack() as es2:
                wd = [g.sb(es2, "wd%d" % i, [128, 22, 128], BF16) for i in range(2)]
                hc = [g.sb(es2, "hc%d" % i, [128, T], F32) for i in range(2)]
                n2 = 0
                for d in range(8):
                    i = d % 2
                    self.wslab(wd[i][:], wd_d[:, d * 128:(d + 1) * 128], "wd%d" % i)
                    self.DMA("sp", hc[i][:], g.hs[b][:, d, :], [], ["hc%d" % i])
                    for ti, (t0, t1) in enumerate(TILES):
                        w = t1 - t0
                        cl = cls_of(t0, b)
                        j = 4 + n2 % 2
                        n2 += 1
                        for f in range(22):
                            self.MM(g.ps[j][:, :w], wd[i][:, f, :], hT[:, f, t0:t1], f == 0, f == 21,
                                    ["wd%d" % i], ["ps%d" % j])
                        self.STT(hc[i][:, t0:t1], g.ps[j][:, :w], g.Gt[:, l, n, d, cl:cl + 1], hc[i][:, t0:t1],
                                 ALU.mult, ALU.add, ["ps%d" % j, "hc%d" % i], ["hc%d" % i])
                    self.DMA("act", g.hs[b][:, d, :], hc[i][:], ["hc%d" % i], [("hs", d)])
                self.S.emit("ffn.2")

    def phase_branch_out(self, l, b, w_d, br, first):
        g = self.g
        w_in = g.w["w_in"][l]
        with ExitStack() as es:
            wo = [g.sb(es, "wo%d" % i, [128, 8, 128], BF16) for i in range(2)]
            wgt = [g.sb(es, "wgt%d" % i, [128, 8, 128], BF16) for i in range(2)]
            sig = [g.sb(es, "sig%d" % i, [128, 512], F32) for i in range(2)]
            pr = [g.sb(es, "pr%d" % i, [128, 512], BF16) for i in range(2)]
            mgc = [g.sb(es, "mgc%d" % i, [128, T], BF16) for i in range(2)]
            n2 = 0
            for d in range(8):
                i = d % 2
                self.wslab(wo[i][:], w_d[:, d * 128:(d + 1) * 128], "wo%d" % i)
                gc = O_G + br * D + d * 128
                self.wslab(wgt[i][:], w_in[:, gc:gc + 128], "wgt%d" % i)
                if not first:
                    self.DMA("pool", mgc[i][:], g.mg_d[d], [], [("mgc", i, ti) for ti in range(5)])
                for ti, (t0, t1) in enumerate(TILES):
                    w = t1 - t0
                    j = n2 % 2
                    n2 += 1
                    pa, pb = g.ps[j * 2], g.ps[j * 2 + 1]
                    for k in range(8):
                        self.MM(pa[:, :w], wo[i][:, k, :], g.yX[:, k, t0:t1], k == 0, k == 7, ["wo%d" % i],
                                ["ps%d" % (j * 2)])
                    for k in range(8):
                        self.MM(pb[:, :w], wgt[i][:, k, :], g.aT[:, k, t0:t1], k == 0, k == 7, ["wgt%d" % i],
                                ["ps%d" % (j * 2 + 1)])
                    self.ACT(sig[j][:, :w], pb[:, :w], AF.Sigmoid, ["ps%d" % (j * 2 + 1)], ["sig%d" % j])
                    if first:
                        self.TT("dve", mgc[i][:, t0:t1], sig[j][:, :w], pa[:, :w], ALU.mult,
                                ["sig%d" % j, "ps%d" % (j * 2)], [("mgc", i, ti)])
                    else:
                        self.TT("dve", pr[j][:, :w], sig[j][:, :w], pa[:, :w], ALU.mult,
                                ["sig%d" % j, "ps%d" % (j * 2)], ["pr%d" % j])
                        self.TT("dve", mgc[i][:, t0:t1], mgc[i][:, t0:t1], pr[j][:, :w], ALU.add,
                                ["pr%d" % j, ("mgc", i, ti)], [("mgc", i, ti)])
                self.DMA("sp", g.mg_d[d], mgc[i][:], [("mgc", i, ti) for ti in range(5)], [("mg_d", d)])
            self.S.emit("branch_out.1")

    def phase_wo(self, l, b):
        g = self.g
        with ExitStack() as es:
            wo = [g.sb(es, "wo%d" % i, [128, 8, 128], BF16) for i in range(2)]
            hc = [g.sb(es, "hc%d" % i, [128, T], F32) for i in range(2)]
            mg = g.sb(es, "mgall", [128, 8, T], BF16)
            for k in range(8):
                self.DMA("sp", mg[:, k, :], g.mg_d[k], [], [("mg", k)])
            mgk = [("mg", k) for k in range(8)]
            n2 = 0
            for d in range(8):
                i = d % 2
                self.wslab(wo[i][:], g.w["w_o"][l][:, d * 128:(d + 1) * 128], "wo%d" % i)
                self.DMA("sp", hc[i][:], g.hs[b][:, d, :], [], ["hc%d" % i])
                for ti, (t0, t1) in enumerate(TILES):
                    w = t1 - t0
                    cl = cls_of(t0, b)
                    j = n2 % 2
                    n2 += 1
                    for k in range(8):
                        self.MM(g.ps[j][:, :w], wo[i][:, k, :], mg[:, k, t0:t1], k == 0, k == 7, ["wo%d" % i, ("mg", k)],
                                ["ps%d" % j])
                    self.STT(hc[i][:, t0:t1], g.ps[j][:, :w], g.Gt[:, l, 1, d, cl:cl + 1], hc[i][:, t0:t1],
                             ALU.mult, ALU.add, ["ps%d" % j, "hc%d" % i], ["hc%d" % i])
                self.DMA("act", g.hs[b][:, d, :], hc[i][:], ["hc%d" % i], [("hs", d)])
            self.S.emit("wo.1")

    def phase_sconv(self, l, b):
        g = self.g
        w_in = g.w["w_in"][l]
        with ExitStack() as es:
            wc = [g.sb(es, "wc%d" % i, [128, 8, 384], BF16) for i in range(2)]
            u = [g.sb(es, "u%d" % i, [128, T], F32) for i in range(2)]
            gb = [g.sb(es, "gb%d" % i, [128, T], F32) for i in range(2)]
            yv = [g.sb(es, "yv%d" % i, [128, T], F32) for i in range(2)]
            ctm = [g.sb(es, "ctm%d" % i, [128, 512], F32) for i in range(2)]
            n2 = 0
            for jc in range(8):
                i = jc % 2
                for s in range(3):
                    c0 = O_CV + s * D + jc * 128
                    self.wslab(wc[i][:, :, s * 128:(s + 1) * 128], w_in[:, c0:c0 + 128], ("wc", i, s))
                for ti, (t0, t1) in enumerate(TILES):
                    w = t1 - t0
                    j = n2 % 2
                    n2 += 1
                    pbk = [g.ps[j * 3 + s] for s in range(3)]
                    for s in range(3):
                        for k in range(8):
                            self.MM(pbk[s][:, :w], wc[i][:, k, s * 128:(s + 1) * 128], g.aT[:, k, t0:t1], k == 0, k == 7,
                                    [("wc", i, s)], ["ps%d" % (j * 3 + s)])
                    self.ACT(gb[i][:, t0:t1], pbk[0][:, :w], AF.Copy, ["ps%d" % (j * 3)], [("gb", i, ti)])
                    self.ACT(ctm[j][:, :w], pbk[1][:, :w], AF.Copy, ["ps%d" % (j * 3 + 1)], ["ctm%d" % j])
                    self.TT("dve", u[i][:, t0:t1], ctm[j][:, :w], pbk[2][:, :w], ALU.mult,
                            ["ctm%d" % j, "ps%d" % (j * 3 + 2)], [("u", i, ti)])
                uk = [("u", i, ti) for ti in range(5)]
                o = g.coff[("cvw", l)] + jc * 3
                w0, w1, w2 = (g.cols[:, o + t:o + t + 1] for t in range(3))
                for (s0, L) in ((0, NCTX), (NCTX, NLAT)):
                    yk = ("yv", i, s0)
                    self.TS("dve", yv[i][:, s0:s0 + L], u[i][:, s0:s0 + L], w1, ALU.mult, uk, [yk])
                    self.STT(yv[i][:, s0 + 1:s0 + L], u[i][:, s0:s0 + L - 1], w0, yv[i][:, s0 + 1:s0 + L],
                             ALU.mult, ALU.add, uk + [yk], [yk])
                    self.STT(yv[i][:, s0:s0 + L - 1], u[i][:, s0 + 1:s0 + L], w2, yv[i][:, s0:s0 + L - 1],
                             ALU.mult, ALU.add, uk + [yk], [yk])
                self.TT("dve", g.yX[:, jc, :], gb[i][:], yv[i][:], ALU.mult,
                        [("gb", i, ti) for ti in range(5)] + [("yv", i, 0), ("yv", i, NCTX)], [("yX", jc)])
            self.S.emit("sconv.1")

    def phase_pool(self, l, b):
        g = self.g
        w_in = g.w["w_in"][l]
        PADT = T + 32
        SEQ = ((8, 0, NCTX), (8 + NCTX + 16, NCTX, NLAT))

        def poff(t0):
            return 8 + t0 if t0 < NCTX else 8 + NCTX + 16 + (t0 - NCTX)
        with ExitStack() as es:
            wp = [g.sb(es, "wp%d" % i, [128, 8, 256], BF16) for i in range(2)]
            pw = [g.sb(es, "pw%d" % i, [128, 2, 256], BF16) for i in range(2)]
            ic = [g.sb(es, "ic%d" % i, [128, T], F32) for i in range(2)]
            vp = [g.sb(es, "vp%d" % i, [128, 2, PADT], F32) for i in range(2)]
            s1 = g.sb(es, "s1", [128, 2, PADT], F32)
            s2 = g.sb(es, "s2", [128, 2, PADT], F32)
            pT = [g.sb(es, "pT%d" % i, [128, 2, T], BF16) for i in range(2)]
            for i_ in range(2):
                self.MEMSET("pool", vp[i_][:], 0.0, ["vp%d" % i_])
            n2 = [0]

            def proj(gi):
                i = gi % 2
                c0 = O_PV + gi * 256
                self.wslab(wp[i][:], w_in[:, c0:c0 + 256], "wp%d" % i)
                self.wslab(pw[i][:], g.w["pool_w"][l, gi], "pw%d" % i)
                self.DMA("sp", ic[i][:], g.invc_d[gi:gi + 1, :].partition_broadcast(128), [], ["ic%d" % i])
                for c in range(2):
                    for ti, (t0, t1) in enumerate(TILES):
                        w = t1 - t0
                        j = n2[0] % 2
                        n2[0] += 1
                        for k in range(8):
                            self.MM(g.ps[j][:, :w], wp[i][:, k, c * 128:(c + 1) * 128], g.aT[:, k, t0:t1], k == 0, k == 7,
                                    ["wp%d" % i], ["ps%d" % j])
                        po = poff(t0)
                        self.ACT(vp[i][:, c, po:po + w], g.ps[j][:, :w], AF.Copy, ["ps%d" % j], ["vp%d" % i])

            def window(gi):
                i = gi % 2
                win = (2, 4, 8, 16)[gi]
                left = win // 2
                src, srck = vp[i], "vp%d" % i
                bufs = [(s1, "s1"), (s2, "s2")]
                step = 1
                bi = 0
                valid = PADT
                while step < win:
                    dst, dstk = bufs[bi]
                    bi ^= 1
                    n_ = valid - step
                    self.TT("dve", dst[:, :, 0:n_], src[:, :, 0:n_], src[:, :, step:step + n_], ALU.add, [srck], [dstk])
                    src, srck = dst, dstk
                    valid = n_
                    step *= 2
                oth, othk = bufs[bi]
                for c in range(2):
                    for (po, s0, L) in SEQ:
                        self.TT("dve", oth[:, c, :L], src[:, c, po - left:po - left + L], ic[i][:, s0:s0 + L], ALU.mult,
                                [srck, "ic%d" % i], [othk])
                        self.TT("dve", pT[i][:, c, s0:s0 + L], oth[:, c, :L], vp[i][:, c, po:po + L], ALU.subtract,
                                [othk, "vp%d" % i], ["pT%d" % i])

            def outp(gi):
                i = gi % 2
                for oc in range(2):
                    o = g.coff[("psc", l)] + gi * 2 + oc
                    for ti, (t0, t1) in enumerate(TILES):
                        w = t1 - t0
                        j = 4 + n2[0] % 2
                        n2[0] += 1
                        for c in range(2):
                            self.MM(g.ps[j][:, :w], pw[i][:, c, oc * 128:(oc + 1) * 128], pT[i][:, c, t0:t1], c == 0, c == 1,
                                    ["pw%d" % i, "pT%d" % i], ["ps%d" % j])
                        self.ACT(g.yX[:, gi * 2 + oc, t0:t1], g.ps[j][:, :w], AF.Identity, ["ps%d" % j],
                                 [("yX", gi * 2 + oc, ti)], scale=g.cols[:, o:o + 1])
            proj(0)
            for gi in range(4):
                if gi + 1 < 4:
                    proj(gi + 1)
                window(gi)
                outp(gi)
            self.S.emit("pool.1")

    def phase_mla(self, l, b):
        g = self.g
        w_in = g.w["w_in"][l]
        aTk = []
        with ExitStack() as es:
            wuq = g.sb(es, "wuq", [128, 3, 1536], BF16)
            wuk = g.sb(es, "wuk", [128, 2, 1024], BF16)
            wuv = g.sb(es, "wuv", [128, 2, 1024], BF16)
            rope = g.sb(es, "rope", [64, 2, T], BF16)
            rotf = g.sb(es, "rotf", [64, 64], F32)
            rotb = g.sb(es, "rotb", [64, 64], BF16)
            cq = g.sb(es, "cq", [128, 3, T], BF16)
            ckv = g.sb(es, "ckv", [128, 2, T], BF16)
            kpe = g.sb(es, "kpe", [128, T], BF16)
            rf = g.sb(es, "rf", [64, 512], F32)
            rb = g.sb(es, "rb", [64, 512], BF16)
            r1 = g.sb(es, "r1", [64, 512], F32)
            r2 = g.sb(es, "r2", [64, 512], F32)
            es1 = ExitStack()
            wq_in = g.sb(es1, "wq_in", [128, 8, 384], BF16)
            wkv_in = g.sb(es1, "wkv_in", [128, 8, 256], BF16)
            wkr_in = g.sb(es1, "wkr_in", [128, 8, 64], BF16)
            sqm = g.sb(es1, "sqm", [128, 3, 512], BF16)
            rsm = g.sb(es1, "rsm", [128, 512], F32)
            self.MEMSET("dve", kpe[64:128, :], 0.0, ["kpez"])
            self.wslab(wq_in[:], w_in[:, O_QD:O_QD + 384], "wq_in")
            self.wslab(wkv_in[:], w_in[:, O_KVD:O_KVD + 256], "wkv_in")
            self.wslab(wkr_in[:], w_in[:, O_KR:O_KR + 64], "wkr_in")
            self.wslab(wuq[:], g.w["mla_w_uq"][l], "wuq")
            self.wslab(wuk[:], g.w["mla_w_uk"][l], "wuk")
            self.wslab(wuv[:], g.w["mla_w_uv"][l], "wuv")
            self.DMA("pool", rope[:], g.rope_d.rearrange("a p t -> p a t"), [], ["rope"])
            self.DMA("sp", rotf[:], g.rot_d, [], ["rotf"])
            self.CP("dve", rotb[:], rotf[:], ["rotf"], ["rotb"])

            def rope_apply(src_ps, dst, t0, t1, pskey, psr, dkey, part1=False, stage=0):
                w = t1 - t0
                if stage in (0, 1):
                    self.ACT(rf[:, :w], src_ps[0:64, :w], AF.Copy, [pskey], ["rf"])
                    self.CP("dve", rb[:, :w], rf[:, :w], ["rf"], ["rb"])
                if stage in (0, 2):
                    self.MM(g.ps[psr][0:64, :w], rotb[:], rb[:, :w], True, True, ["rotb", "rb"], ["ps%d" % psr])
                    self.TT("dve", r1[:, :w], rf[:, :w], rope[:, 0, t0:t1], ALU.mult, ["rf", "rope"], ["r1"])
                    self.TT("dve", r2[:, :w], g.ps[psr][0:64, :w], rope[:, 1, t0:t1], ALU.mult, ["ps%d" % psr, "rope"], ["r2"])
                    self.TT(ROPE_Q, dst[0:64, t0:t1], r1[:, :w], r2[:, :w], ALU.add, ["r1", "r2"], [dkey])

            sqk = g.sb(es1, "sqk", [128, 2, 512], BF16)
            rsk = g.sb(es1, "rsk", [128, 512], F32)
            for ti, (t0, t1) in enumerate(TILES):
                w = t1 - t0
                for c in range(3):
                    for k in range(8):
                        self.MM(g.ps[c][:, :w], wq_in[:, k, c * 128:(c + 1) * 128], g.aT[:, k, t0:t1], k == 0, k == 7,
                                ["wq_in"], ["ps%d" % c])
                for c in range(2):
                    for k in range(8):
                        self.MM(g.ps[6 + c][:, :w], wkv_in[:, k, c * 128:(c + 1) * 128], g.aT[:, k, t0:t1], k == 0, k == 7,
                                ["wkv_in"], ["ps%d" % (6 + c)])
                for k in range(8):
                    self.MM(g.ps[4][0:64, :w], wkr_in[:, k, :], g.aT[:, k, t0:t1], k == 0, k == 7, ["wkr_in"], ["ps4"])
                for c in range(3):
                    self.ACT(sqm[:, c, :w], g.ps[c][:, :w], AF.Square, ["ps%d" % c], [("sqm", c)])
                for c in range(2):
                    self.ACT(sqk[:, c, :w], g.ps[6 + c][:, :w], AF.Square, ["ps%d" % (6 + c)], [("sqk", c)])
                rope_apply(g.ps[4], kpe, t0, t1, "ps4", 5, ("kpe", ti), part1=True, stage=1)
                for c in range(3):
                    self.MM(g.ps[3][:, :w], g.onesb[:], sqm[:, c, :w], c == 0, c == 2, [("sqm", c)], ["ps3"])
                self.ACT(rsm[:, :w], g.ps[3][:, :w], AF.Sqrt, ["ps3"], ["rsm"], bias=EPS, scale=1.0 / 384)
                self.RECIP(rsm[:, :w], rsm[:, :w], ["rsm"], ["rsm"])
                for c in range(2):
                    self.MM(g.ps[3][:, :w], g.onesb[:], sqk[:, c, :w], c == 0, c == 1, [("sqk", c)], ["ps3"])
                self.ACT(rsk[:, :w], g.ps[3][:, :w], AF.Sqrt, ["ps3"], ["rsk"], bias=EPS, scale=1.0 / 256)
                self.RECIP(rsk[:, :w], rsk[:, :w], ["rsk"], ["rsk"])
                for c in range(3):
                    self.STT(cq[:, c, t0:t1], g.ps[c][:, :w], col(g, "qn", l, c), rsm[:, :w], ALU.mult, ALU.mult,
                             ["ps%d" % c, "rsm"], [("cq", c, ti)])
                for c in range(2):
                    self.STT(ckv[:, c, t0:t1], g.ps[6 + c][:, :w], col(g, "kvn", l, c), rsk[:, :w], ALU.mult, ALU.mult,
                             ["ps%d" % (6 + c), "rsk"], [("ckv", c, ti)])
                rope_apply(g.ps[4], kpe, t0, t1, "ps4", 5, ("kpe", ti), part1=True, stage=2)
            self.S.emit("mla.1")
            es1.close()
            if MLA_PARTS == 1:
                return
            qn = [g.sb(es, "qn%d" % i, [128, T], BF16) for i in range(2)]
            qpe = [g.sb(es, "qpe%d" % i, [128, T], BF16) for i in range(2)]
            for i_ in range(2):
                self.MEMSET("dve", qpe[i_][64:128, :], 0.0, [("qpez", i_)])
            kn = [g.sb(es, "kn%d" % i, [128, T], BF16) for i in range(2)]
            Vh = [g.sb(es, "Vh%d" % i, [128, NCH, 128], BF16) for i in range(2)]
            PT = [g.sb(es, "PT%d" % i, [128, 512], BF16) for i in range(4)]
            rz = [g.sb(es, "rz%d" % i, [128, 512], F32) for i in range(2)]
            zsb = [g.sb(es, "zsb%d" % i, [128, 512], F32) for i in range(2)]
            osb = [g.sb(es, "osb%d" % i, [128, 512], F32) for i in range(2)]
            cqk = []
            ckvk = []
            kpek = []
            scale = 192.0 ** -0.5
            nacc = 0
            npt = 0
            for h in range(NHEADS_DBG):
                i = h % 2
                for ti, (t0, t1) in enumerate(TILES):
                    w = t1 - t0
                    for c in range(3):
                        self.MM(g.ps[QB][0:64, :w], wuq[:, c, h * 192 + QOFF:h * 192 + QOFF + 64], cq[:, c, t0:t1], c == 0, c == 2,
                                ["wuq"] + cqk, ["ps%d" % QB])
                    rope_apply(g.ps[QB], qpe[i], t0, t1, "ps%d" % QB, 5, ("qpe", i, ti), stage=1)
                    for c in range(3):
                        self.MM(g.ps[6][:, :w], wuq[:, c, h * 192:h * 192 + 128], cq[:, c, t0:t1], c == 0, c == 2,
                                ["wuq"] + cqk, ["ps6"])
                    self.ACT(qn[i][:, t0:t1], g.ps[6][:, :w], AF.Copy, ["ps6"], [("qn", i, ti)])
                    for c in range(2):
                        self.MM(g.ps[6][:, :w], wuk[:, c, h * 128:(h + 1) * 128], ckv[:, c, t0:t1], c == 0, c == 1,
                                ["wuk"] + ckvk, ["ps6"])
                    self.CP("dve", kn[i][:, t0:t1], g.ps[6][:, :w], ["ps6"], [("kn", i, ti)])
                    rope_apply(g.ps[QB], qpe[i], t0, t1, "ps%d" % QB, 5, ("qpe", i, ti), stage=2)
                for kq in range(0, NCH, 4):
                    if "d" in MLA_SKIP:
                        break
                    nk = min(4, NCH - kq)
                    for kk in range(nk):
                        kt = kq + kk
                        for c in range(2):
                            self.MM(g.ps[7][:, kk * 128:(kk + 1) * 128], ckv[:, c, kt * 128:(kt + 1) * 128],
                                    wuv[:, c, h * 128:(h + 1) * 128], c == 0, c == 1, ["wuv"] + ckvk, ["ps7"])
                    self.ACT(Vh[i][:, kq:kq + nk, :], g.ps[7][:, :nk * 128].rearrange("p (a b) -> p a b", a=nk), AF.Copy,
                             ["ps7"], [("Vh", i, kq)])
                for qi, (q0, q1) in enumerate(TILES):
                    if MLA_PARTS == 2:
                        break
                    w = q1 - q0
                    keys = [0, 1] if qi == 0 else list(range(NCH))
                    ja = nacc % 2
                    nacc += 1
                    pO, pZ = g.ps[2], g.ps[3]
                    kO, kZ = "ps2", "ps3"
                    PSB = (0, 1, 4)

                    def smm(kt, pj):
                        self.MM(g.ps[pj][:, :w], kn[i][:, kt * 128:(kt + 1) * 128], qn[i][:, q0:q1], True, False,
                                [("kn", i, tile_of(kt)), ("qn", i, qi)], ["ps%d" % pj])
                        self.MM(g.ps[pj][:, :w], kpe[:, kt * 128:(kt + 1) * 128], qpe[i][:, q0:q1], False, True,
                                kpek + [("qpe", i, qi)], ["ps%d" % pj])
                    m0 = npt
                    for la in range(min(2, len(keys))):
                        smm(keys[la], PSB[(m0 + la) % 3])
                    for n_, kt in enumerate(keys):
                        pj = PSB[npt % 3]
                        pt = npt % 4
                        npt += 1
                        if n_ + 2 < len(keys):
                            smm(keys[n_ + 2], PSB[(m0 + n_ + 2) % 3])
                        self.ACT(PT[pt][:, :w], g.ps[pj][:, :w], AF.Exp, ["ps%d" % pj], ["PT%d" % pt], scale=scale)
                        self.MM(pO[:, :w], Vh[i][:, kt, :], PT[pt][:, :w], n_ == 0, n_ == len(keys) - 1,
                                [("Vh", i, (kt // 4) * 4), "PT%d" % pt], [kO])
                        self.MM(pZ[:, :w], g.onesb[:], PT[pt][:, :w], n_ == 0, n_ == len(keys) - 1, ["PT%d" % pt], [kZ])
                    self.ACT(zsb[ja][:, :w], pZ[:, :w], AF.Copy, [kZ], ["zsb%d" % ja])
                    self.CP("dve", osb[ja][:, :w], pO[:, :w], [kO], ["osb%d" % ja])
                    self.RECIP(rz[ja][:, :w], zsb[ja][:, :w], ["zsb%d" % ja], ["rz%d" % ja])
                    self.TT("pool", g.yX[:, h, q0:q1], osb[ja][:, :w], rz[ja][:, :w], ALU.mult, ["osb%d" % ja, "rz%d" % ja],
                            [("yX", h, qi)])
            self.S.emit("mla.2")

    def phase_ssd(self, l, b, stop_after, tag):
        g = self.g
        w_in = g.w["w_in"][l]
        with ExitStack() as es:
            wx = [g.sb(es, "wx%d" % i, [128, 8, 128], BF16) for i in range(2)]
            raw = [g.sb(es, "raw%d" % i, [128, T], F32) for i in range(2)]
            yv = [g.sb(es, "yc%d" % i, [128, T], F32) for i in range(2)]
            ob = [g.sb(es, "ob%d" % i, [128, T], BF16) for i in range(2)]
            n2 = [0]

            def st1(cc):
                i = cc % 2
                c0 = O_XBC + cc * 128
                self.wslab(wx[i][:], w_in[:, c0:c0 + 128], "wx%d" % i)
                for ti, (t0, t1) in enumerate(TILES):
                    w = t1 - t0
                    j = n2[0] % 2
                    n2[0] += 1
                    for k in range(8):
                        self.MM(g.ps[j][:, :w], wx[i][:, k, :], g.aT[:, k, t0:t1], k == 0, k == 7, ["wx%d" % i], ["ps%d" % j])
                    self.ACT(raw[i][:, t0:t1], g.ps[j][:, :w], AF.Copy, ["ps%d" % j], [("raw", i, ti)])

            def st2(cc):
                i = cc % 2
                rk = [("raw", i, ti) for ti in range(5)]
                o = g.coff[("scw", l)] + cc * 4
                w0, w1, w2, w3 = (g.cols[:, o + t:o + t + 1] for t in range(4))
                for (s0, L) in ((0, NCTX), (NCTX, NLAT)):
                    yk = ("yc", i, s0)
                    self.TS("dve", yv[i][:, s0:s0 + L], raw[i][:, s0:s0 + L], w2, ALU.mult, rk, [yk])
                    self.STT(yv[i][:, s0 + 2:s0 + L], raw[i][:, s0:s0 + L - 2], w0, yv[i][:, s0 + 2:s0 + L],
                             ALU.mult, ALU.add, rk + [yk], [yk])
                    self.STT(yv[i][:, s0 + 1:s0 + L], raw[i][:, s0:s0 + L - 1], w1, yv[i][:, s0 + 1:s0 + L],
                             ALU.mult, ALU.add, rk + [yk], [yk])
                    self.STT(yv[i][:, s0:s0 + L - 1], raw[i][:, s0 + 1:s0 + L], w3, yv[i][:, s0:s0 + L - 1],
                             ALU.mult, ALU.add, rk + [yk], [yk])

            def st3(cc):
                i = cc % 2
                self.ACT(ob[i][:], yv[i][:], AF.Silu, [("yc", i, 0), ("yc", i, NCTX)], ["ob%d" % i],
                         bias=col(g, "scb", l, cc))
                self.DMA("sp", g.xbcT[cc], ob[i][:], ["ob%d" % i], [("xbcT", cc)])
            for cc in range(16):
                st1(cc)
                if cc >= 1:
                    st3(cc - 1)
                st2(cc)
            st3(15)
            self.S.emit("ssd.1")
        if self.done("ssd_conv" + tag, stop_after, (g.xbcT, "sp")):
            return True

        with ExitStack() as es:
            f = lambda name, shape, dt=F32: g.sb(es, name, shape, dt)
            wdt = f("wdt", [128, 8, 32], BF16)
            rowb = f("rowb", [128, 96])
            dt = f("dt", [128, NCH, 32])
            da = f("da", [128, NCH, 32])
            cs = f("cs", [128, NCH, 32])
            tot = f("tot", [128, NCH, 32])
            dte = f("dte", [128, NCH, 32])
            ecs = f("ecs", [128, NCH, 32])
            etot = f("etot", [128, NCH, 32])
            sdd = f("sdd", [128, NCH, 32])
            self.wslab(wdt[:], w_in[:, O_DT:O_DT + 32], "wdt")
            self.DMA("sp", rowb[:], g.rows_d[l:l + 1].rearrange("a r n -> a (r n)").partition_broadcast(128), [], ["rowb"])
            for (c0, c1, bk) in ((0, 16, 0), (16, 18, 1)):
                for c in range(c0, c1):
                    for k in range(8):
                        self.MM(g.ps[bk][:, (c - c0) * 32:(c - c0 + 1) * 32], g.aT[:, k, c * 128:(c + 1) * 128], wdt[:, k, :],
                                k == 0, k == 7, ["wdt"], ["ps%d" % bk])
                nn = c1 - c0
                self.TT("dve", dt[:, c0:c1, :], g.ps[bk][:, :nn * 32].rearrange("p (a b) -> p a b", a=nn),
                        rowb[:, 0:32].unsqueeze(1).broadcast_to([128, nn, 32]), ALU.add, ["ps%d" % bk, "rowb"], [("dt", bk)])
            dtk = [("dt", 0), ("dt", 1)]
            self.ACT(dt[:], dt[:], AF.Exp, dtk, ["dt"])
            self.ACT(dt[:], dt[:], AF.Ln, ["dt"], ["dt"], bias=1.0)
            self.ACT(rowb[:, 32:64], rowb[:, 32:64], AF.Exp, ["rowb"], ["rowa"])
            self.TS("dve", rowb[:, 32:64], rowb[:, 32:64], -1.0, ALU.mult, ["rowa"], ["rowa"])
            self.TT("dve", da[:], dt[:], rowb[:, 32:64].unsqueeze(1).broadcast_to([128, NCH, 32]), ALU.mult, ["dt", "rowa"], ["da"])
            for d in range(2):
                tri = g.Lf if d == 0 else g.Ub
                self.MM(g.ps[2 + d][:, :NCH * 16], tri, da[:, :, d * 16:(d + 1) * 16], True, True, ["da", "consts"],
                        ["ps%d" % (2 + d)])
                self.CP("dve", cs[:, :, d * 16:(d + 1) * 16], g.ps[2 + d][:, :NCH * 16].rearrange("p (a b) -> p a b", a=NCH),
                        ["ps%d" % (2 + d)], [("cs", d)])
                self.MM(g.ps[4 + d][:, :NCH * 16], g.onesf, da[:, :, d * 16:(d + 1) * 16], True, True, ["da", "consts"],
                        ["ps%d" % (4 + d)])
                self.CP("dve", tot[:, :, d * 16:(d + 1) * 16], g.ps[4 + d][:, :NCH * 16].rearrange("p (a b) -> p a b", a=NCH),
                        ["ps%d" % (4 + d)], [("tot", d)])
            csk = [("cs", 0), ("cs", 1)]
            totk = [("tot", 0), ("tot", 1)]
            self.TT("dve", dte[:], tot[:], cs[:], ALU.subtract, csk + totk, ["dte"])
            self.ACT(dte[:], dte[:], AF.Exp, ["dte"], ["dte"])
            self.ACT(ecs[:], cs[:], AF.Exp, csk, ["ecs"])
            self.ACT(etot[:], tot[:], AF.Exp, totk, ["etot"])
            self.TT("dve", sdd[:], dt[:], dte[:], ALU.mult, ["dt", "dte"], ["sdd"])
            self.S.emit("ssd.2")
            if stop_after == "ssd_dt" + tag:
                self.DMA("sp", g.dbg[0], dt[:], [], [])
                self.DMA("sp", g.dbg[1], cs[:], [], [])
                self.DMA("sp", g.dbg[2], tot[:], [], [])
                self.S.emit("ssd.3")
                return True

            def f2(name, shape, dt=F32):
                return [f("%s%d" % (name, i), shape, dt) for i in range(2)]
            xc = f2("xc", [128, 16, 128], BF16)
            xs_tok = f2("xs_tok", [128, 16, 64], BF16)
            B_tok = f2("B_tok", [128, 4, 128], BF16)
            xdd = f2("xdd", [128, 16, 64], BF16)
            xdt = f2("xdt", [128, 16, 64], BF16)
            Sst = f("Sst", [128, 16, 64])
            Sbf = f2("Sbf", [128, D], BF16)
            Sbw = f2("Sbw", [128, D], BF16)
            Gmf = f2("Gmf", [128, 4, 128])
            Gmb = f2("Gmb", [128, 4, 128])
            yacc = f2("yacc", [128, 16, 64])
            ytmp = f2("ytmp", [128, 16, 64])
            rseg = f2("rseg", [128, 4, 128])
            Eb = f2("Eb", [128, 4, 128])
            Mt = f2("Mt", [128, 4, 128], BF16)
            yzn = f2("yzn", [128, D], BF16)

            def load_a(c, i):
                self.DMA("sp", xc[i][:], g.xbcT[:, :, c * 128:(c + 1) * 128].rearrange("c p t -> p c t"), [], ["xc%d" % i])
                pxs = g.ps[0][:].bitcast(BF16)
                pbt = g.ps[2][:].bitcast(BF16)
                for k in range(8):
                    self.TR(pxs[:, k * 128:(k + 1) * 128], xc[i][:, k, :], g.identb[:], ["xc%d" % i, "identb"], ["ps0"])
                for gq in range(4):
                    self.TR(pbt[:, gq * 128:(gq + 1) * 128], xc[i][:, 8 + gq, :], g.identb[:], ["xc%d" % i, "identb"], ["ps2"])

            def load_b(c, i):
                pxs = g.ps[0][:].bitcast(BF16)
                pbt = g.ps[2][:].bitcast(BF16)
                self.CP("dve", xs_tok[i][:].rearrange("p a b -> p (a b)"), pxs[:, :], ["ps0"], ["xs_tok%d" % i])
                self.ACT(B_tok[i][:].rearrange("p a b -> p (a b)"), pbt[:, 0:512], AF.Copy, ["ps2"], ["B_tok%d" % i])

            def load_chunk(c, i):
                load_a(c, i)
                load_b(c, i)

            def local_mm(c, d, i, bp):
                self.TT("pool", xdd[i][:], xs_tok[i][:], sdd[:, c, d * 16:(d + 1) * 16].unsqueeze(2).broadcast_to([128, 16, 64]),
                        ALU.mult, ["xs_tok%d" % i, "sdd"], ["xdd%d" % i])
                for gq in range(4):
                    bk = bp + gq // 2
                    self.MM(g.ps[bk][:, (gq % 2) * 256:(gq % 2 + 1) * 256], B_tok[i][:, gq, :],
                            xdd[i][:, gq * 4:(gq + 1) * 4, :].rearrange("p a b -> p (a b)"), True, True,
                            ["B_tok%d" % i, "xdd%d" % i], ["ps%d" % bk])

            def state_apply(c, d, bp):
                self.TT("dve", Sst[:], Sst[:], etot[:, c, d * 16:(d + 1) * 16].unsqueeze(2).broadcast_to([128, 16, 64]),
                        ALU.mult, ["Sst", "etot"], ["Sst"])
                for hf in range(2):
                    self.TT("dve", Sst[:, hf * 8:(hf + 1) * 8, :], Sst[:, hf * 8:(hf + 1) * 8, :],
                            g.ps[bp + hf][:].rearrange("p (a b) -> p a b", a=8), ALU.add, ["Sst", "ps%d" % (bp + hf)], ["Sst"])

            def state_update(c, d, i):
                local_mm(c, d, i, 6)
                state_apply(c, d, 6)

            self.MEMSET("pool", Sst[:], 0.0, ["Sst"])
            order_b = [1, 0] + list(range(NCH - 1, 1, -1))

            load_chunk(order_b[0], 0)
            local_mm(order_b[0], 1, 0, 4)
            for n_, c in enumerate(order_b):
                i = n_ % 2
                nxt = n_ + 1 < NCH - 1
                if nxt:
                    load_a(order_b[n_ + 1], (n_ + 1) % 2)
                self.ACT(Sbf[i][:], Sst[:].rearrange("p a b -> p (a b)"), AF.Copy, ["Sst"], ["Sbf%d" % i])
                self.DMA("sp", g.sbwd[c], Sbf[i][:], ["Sbf%d" % i], [("sbwd", c)])
                if n_ == NCH - 1:
                    break
                state_apply(c, 1, 4 + 2 * (n_ % 2))
                if nxt:
                    load_b(order_b[n_ + 1], (n_ + 1) % 2)
                    local_mm(order_b[n_ + 1], 1, (n_ + 1) % 2, 4 + 2 * ((n_ + 1) % 2))
            self.S.emit("ssd.4")
            if stop_after == "ssd_bwd" + tag:
                self.DMA("sp", g.dbg, g.sbwd, [], [])
                self.S.emit("ssd.5")
                return True

            self.MEMSET("pool", Sst[:], 0.0, ["Sst"])
            nseg = 0
            for c in range(NCH):
                i = c % 2
                xck, xsk, btk = "xc%d" % i, "xs_tok%d" % i, "B_tok%d" % i
                yk = "yacc%d" % i
                self.DMA("sp", Sbw[i][:], g.sbwd[c], [], ["Sbw%d" % i])
                load_chunk(c, i)
                self.ACT(Sbf[i][:], Sst[:].rearrange("p a b -> p (a b)"), AF.Copy, ["Sst"], ["Sbf%d" % i])
                for gq in range(4):
                    self.MM(g.ps[2][:, gq * 128:(gq + 1) * 128], xc[i][:, 8 + gq, :], xc[i][:, 12 + gq, :], True, True,
                            [xck], ["ps2"])
                pG = g.ps[2][:].rearrange("p (a b) -> p a b", a=4)
                self.TT("dve", Gmf[i][:], pG, g.Lf.unsqueeze(1).broadcast_to([128, 4, 128]), ALU.mult, ["ps2", "consts"], ["Gmf%d" % i])
                self.TT("dve", Gmb[i][:], pG, g.Ub.unsqueeze(1).broadcast_to([128, 4, 128]), ALU.mult, ["ps2", "consts"], ["Gmb%d" % i])
                self.TT("pool", yacc[i][:], xs_tok[i][:], rowb[:, 64:80].unsqueeze(2).broadcast_to([128, 16, 64]), ALU.mult,
                        [xsk, "rowb"], [yk])
                for d in range(2):
                    Sp, Spk = (Sbf[i], "Sbf%d" % i) if d == 0 else (Sbw[i], "Sbw%d" % i)
                    for gq in range(4):
                        bk = 6 + gq // 2
                        self.MM(g.ps[bk][:, (gq % 2) * 256:(gq % 2 + 1) * 256], xc[i][:, 12 + gq, :],
                                Sp[:, gq * 256:(gq + 1) * 256], True, True, [xck, Spk], ["ps%d" % bk])
                    for hf in range(2):
                        self.TT("dve", ytmp[d][:, hf * 8:(hf + 1) * 8, :], g.ps[6 + hf][:].rearrange("p (a b) -> p a b", a=8),
                                ecs[:, c, d * 16 + hf * 8:d * 16 + hf * 8 + 8].unsqueeze(2).broadcast_to([128, 8, 64]),
                                ALU.mult, ["ps%d" % (6 + hf), "ecs"], [("ytmp", d, hf)])
                    self.TT("pool", yacc[i][:], yacc[i][:], ytmp[d][:], ALU.add, [yk, ("ytmp", d, 0), ("ytmp", d, 1)], [yk])
                    self.TT("pool", xdt[d][:], xs_tok[i][:], dt[:, c, d * 16:(d + 1) * 16].unsqueeze(2).broadcast_to([128, 16, 64]),
                            ALU.mult, [xsk, "dt"], ["xdt%d" % d])
                items = [(d, gq) for d in range(2) for gq in range(4)]
                SB = (1, 3)

                def seg_issue(n):
                    d, gq = items[n]
                    r = n % 2
                    triR = g.Lf if d == 0 else g.Ub
                    triL = g.SL if d == 0 else g.SU
                    for hh in range(4):
                        hcol = d * 16 + gq * 4 + hh
                        self.ACT(rseg[r][:, hh, :], triR, AF.Copy, ["consts", "da"], [("rseg", r, hh)], scale=da[:, c, hcol:hcol + 1])
                    self.MM(g.ps[SB[r]][:], triL, rseg[r][:].rearrange("p a b -> p (a b)"), True, True,
                            [("rseg", r, hh) for hh in range(4)] + ["consts"], ["ps%d" % SB[r]])
                seg_issue(0)
                for n, (d, gq) in enumerate(items):
                    r = n % 2
                    if n + 1 < len(items):
                        seg_issue(n + 1)
                    Gmd, Gk = (Gmf[i], "Gmf%d" % i) if d == 0 else (Gmb[i], "Gmb%d" % i)
                    self.ACT(Eb[r][:].rearrange("p a b -> p (a b)"), g.ps[SB[r]][:], AF.Exp, ["ps%d" % SB[r]], ["Eb%d" % r])
                    self.TT("dve", Mt[r][:], Eb[r][:], Gmd[:, gq, :].unsqueeze(1).broadcast_to([128, 4, 128]), ALU.mult,
                            ["Eb%d" % r, Gk], ["Mt%d" % r])
                    for hh in range(4):
                        h = gq * 4 + hh
                        bk = 4 + h // 8
                        self.MM(g.ps[bk][:, (h % 8) * 64:(h % 8 + 1) * 64], Mt[r][:, hh, :], xdt[d][:, h, :], True, True,
                                ["Mt%d" % r, "xdt%d" % d], ["ps%d" % bk])
                    if gq == 3:
                        for hf in range(2):
                            self.TT("dve", yacc[i][:, hf * 8:(hf + 1) * 8, :], yacc[i][:, hf * 8:(hf + 1) * 8, :],
                                    g.ps[4 + hf][:].rearrange("p (a b) -> p a b", a=8), ALU.add, [yk, "ps%d" % (4 + hf)], [yk])
                if c < NCH - 1:
                    state_update(c, 0, i)
                self.ACT(yzn[i][:], yacc[i][:].rearrange("p a b -> p (a b)"), AF.Copy, [yk], ["yzn%d" % i])
                pxo = g.ps[2][:].bitcast(BF16)
                for k in range(8):
                    self.TR(pxo[:, k * 128:(k + 1) * 128], yzn[i][:, k * 128:(k + 1) * 128], g.identb[:], ["yzn%d" % i, "identb"], ["ps2"])
                self.CP("dve", g.yX[:, :, c * 128:(c + 1) * 128], pxo.rearrange("p (a b) -> p a b", a=8), ["ps2"], [("yX", c)])
            self.S.emit("ssd.6")

    def phase_ssd_post(self, l, b):
        g = self.g
        w_in = g.w["w_in"][l]
        with ExitStack() as es:
            wz = g.sb(es, "wz", [128, 8, D], BF16)
            sz = [g.sb(es, "sz%d" % i, [128, 512], F32) for i in range(2)]
            gt = [g.sb(es, "gt%d" % i, [128, 8, 512], F32) for i in range(2)]
            sq = [g.sb(es, "sqp%d" % i, [128, 512], BF16) for i in range(4)]
            rs = [g.sb(es, "rsp%d" % i, [128, 512], F32) for i in range(2)]
            self.wslab(wz[:], w_in[:, O_Z:O_Z + D], "wz")
            n2 = 0
            o = g.coff[("ssdn", l)]

            def fin(ti):
                t0, t1 = TILES[ti]
                w = t1 - t0
                i = ti % 2
                for oc in range(8):
                    self.STT(g.yX[:, oc, t0:t1], gt[i][:, oc, :w], g.cols[:, o + oc:o + oc + 1], rs[i][:, :w], ALU.mult, ALU.mult,
                             [("gt", i, oc), "rsp%d" % i, "cols"], [("yXp", oc, ti)])
            nq = [0]
            for ti, (t0, t1) in enumerate(TILES):
                w = t1 - t0
                i = ti % 2
                pq = g.ps[4 + i]
                pend = []

                def ones_mm(oc, qi_):
                    self.MM(pq[:, :w], g.onesb[:], sq[qi_][:, :w], oc == 0, oc == 7, ["sqp%d" % qi_], ["ps%d" % (4 + i)])
                for oc in range(8):
                    j = n2 % 4
                    n2 += 1
                    for k in range(8):
                        self.MM(g.ps[j][:, :w], wz[:, k, oc * 128:(oc + 1) * 128], g.aT[:, k, t0:t1], k == 0, k == 7, ["wz"], ["ps%d" % j])
                    if len(pend) >= 2:
                        ones_mm(*pend.pop(0))
                    qi_ = nq[0] % 4
                    nq[0] += 1
                    self.ACT(sz[j % 2][:, :w], g.ps[j][:, :w], AF.Silu, ["ps%d" % j], ["sz%d" % (j % 2)])
                    self.TT("dve", gt[i][:, oc, :w], sz[j % 2][:, :w], g.yX[:, oc, t0:t1], ALU.mult, ["sz%d" % (j % 2)], [("gt", i, oc)])
                    self.ACT(sq[qi_][:, :w], gt[i][:, oc, :w], AF.Square, [("gt", i, oc)], ["sqp%d" % qi_])
                    pend.append((oc, qi_))
                while pend:
                    ones_mm(*pend.pop(0))
                self.ACT(rs[i][:, :w], pq[:, :w], AF.Sqrt, ["ps%d" % (4 + i)], ["rsp%d" % i], bias=EPS, scale=1.0 / D)
                self.RECIP(rs[i][:, :w], rs[i][:, :w], ["rsp%d" % i], ["rsp%d" % i])
                if ti >= 1:
                    fin(ti - 1)
            fin(len(TILES) - 1)
            self.S.emit("ssd_post.1")

    def phase_final(self):
        g = self.g
        with ExitStack() as es:
            ht = [g.sb(es, "ht%d" % i, [128, 8, 512], F32) for i in range(2)]
            sq = [g.sb(es, "sq%d" % i, [128, 8, 512], BF16) for i in range(2)]
            rs = [g.sb(es, "rs%d" % i, [128, 512], F32) for i in range(2)]
            tmp = [g.sb(es, "tmp%d" % i, [128, 8, 512], F32) for i in range(2)]
            ot = [g.sb(es, "ot%d" % i, [128, D], F32) for i in range(2)]
            n2 = 0
            n3 = 0
            for b in range(g.nb):
                for ti, (t0, t1) in enumerate(TILES[1:]):
                    w = t1 - t0
                    i = n2 % 2
                    n2 += 1
                    self.DMA("sp", ht[i][:], g.hs[b][:, :, t0:t1], [], ["ht%d" % i])
                    self.ACT(sq[i][:], ht[i][:], AF.Square, ["ht%d" % i], ["sq%d" % i])
                    for k in range(8):
                        self.MM(g.ps[i][:], g.onesb[:], sq[i][:, k, :], k == 0, k == 7, ["sq%d" % i], ["ps%d" % i])
                    self.ACT(rs[i][:], g.ps[i][:], AF.Sqrt, ["ps%d" % i], ["rs%d" % i], bias=EPS, scale=1.0 / D)
                    self.RECIP(rs[i][:], rs[i][:], ["rs%d" % i], ["rs%d" % i])
                    self.TT("dve", tmp[i][:], ht[i][:], rs[i][:].unsqueeze(1).broadcast_to([128, 8, 512]),
                            ALU.mult, ["ht%d" % i, "rs%d" % i], ["tmp%d" % i])
                    for k in range(8):
                        o = g.coff["fn"] + k
                        self.ACT(tmp[i][:, k, :], tmp[i][:, k, :], AF.Identity, ["tmp%d" % i], ["tmp%d" % i],
                                 scale=g.cols[:, o:o + 1])
                    for sub in range(4):
                        j = n3 % 2
                        n3 += 1
                        for k in range(8):
                            bk = 2 + j * 2 + k // 4
                            self.TR(g.ps[bk][:, (k % 4) * 128:(k % 4 + 1) * 128], tmp[i][:, k, sub * 128:(sub + 1) * 128],
                                    g.identf, ["tmp%d" % i, "consts"], ["ps%d" % bk])
                        self.ACT(ot[j][:, 0:512], g.ps[2 + j * 2][:], AF.Copy, ["ps%d" % (2 + j * 2)], [("ot", j, 0)])
                        self.CP("dve", ot[j][:, 512:1024], g.ps[3 + j * 2][:], ["ps%d" % (3 + j * 2)], [("ot", j, 1)])
                        tok0 = t0 - NCTX + sub * 128
                        self.DMA("pool", g.out[b, tok0:tok0 + 128, :], ot[j][:], [("ot", j, 0), ("ot", j, 1)], [])
            self.S.emit("final.1")


def host_consts():
    idx = np.arange(128)
    r, c = idx[:, None], idx[None, :]
    mats = [np.eye(128), np.ones((128, 128)), (r <= c), (r >= c), (r > c), (r < c)]
    consts = np.concatenate([m.astype(np.float32) for m in mats], axis=1)
    rows = NLAT // 64
    row = np.repeat(np.arange(rows), 64).astype(np.float32)
    colp = np.tile(np.arange(64), rows).astype(np.float32)
    half = 32
    inv = (1.0 / (np.float32(10000.0) ** (np.arange(0, half, 2, dtype=np.float32) / np.float32(half)))).astype(np.float32)
    ar = row[:, None] * inv
    ac = colp[:, None] * inv
    ang = np.concatenate([ar, ar, ac, ac], axis=-1).astype(np.float32)
    cosT = np.ones((64, T), np.float32)
    sinT = np.zeros((64, T), np.float32)
    cosT[:, NCTX:] = np.cos(ang).T
    sinT[:, NCTX:] = np.sin(ang).T
    rope = np.stack([cosT, sinT]).astype(np.float32)
    rot = np.zeros((64, 64), np.float32)
    for m in range(64):
        blk = m // 16
        if blk % 2 == 0:
            rot[m + 16, m] = -1.0
        else:
            rot[m - 16, m] = 1.0
    invc = np.zeros((4, T), np.float32)
    for gi, win in enumerate((2, 4, 8, 16)):
        left = win // 2
        for (s0, n) in ((0, NCTX), (NCTX, NLAT)):
            t = np.arange(n)
            lo = np.clip(t - left, 0, n)
            hi = np.clip(t - left + win, 0, n)
            invc[gi, s0:s0 + n] = 1.0 / (hi - lo).astype(np.float32)
    return consts, rope, rot, invc


def host_cols(inp, depth):
    off, ncols = cols_layout(depth)
    cols = np.zeros((128, ncols), np.float32)

    def fm(v):
        return np.ascontiguousarray(np.asarray(v, np.float32).reshape(-1, 128).T)
    for l in range(depth):
        cols[:, off[("gam0", l)]:off[("gam0", l)] + 8] = fm(inp["ffn1_norm"][l])
        cols[:, off[("gam1", l)]:off[("gam1", l)] + 8] = fm(inp["mix_norm"][l])
        cols[:, off[("gam2", l)]:off[("gam2", l)] + 8] = fm(inp["ffn2_norm"][l])
        cols[:, off[("bada", l)]:off[("bada", l)] + 72] = fm(inp["b_ada"][l])
        scw = np.asarray(inp["ssd_conv_w"][l], np.float32)
        cols[:, off[("scw", l)]:off[("scw", l)] + 64] = scw.T.reshape(16, 128, 4).transpose(1, 0, 2).reshape(128, 64)
        cols[:, off[("scb", l)]:off[("scb", l)] + 16] = fm(inp["ssd_conv_b"][l])
        cols[:, off[("ssdn", l)]:off[("ssdn", l)] + 8] = fm(inp["ssd_norm"][l])
        cols[:, off[("qn", l)]:off[("qn", l)] + 3] = fm(inp["mla_q_norm"][l])
        cols[:, off[("kvn", l)]:off[("kvn", l)] + 2] = fm(inp["mla_kv_norm"][l])
        cols[:, off[("psc", l)]:off[("psc", l)] + 8] = fm(inp["pool_scale"][l])
        cvw = np.asarray(inp["sconv_w"][l], np.float32)
        cols[:, off[("cvw", l)]:off[("cvw", l)] + 24] = cvw.T.reshape(8, 128, 3).transpose(1, 0, 2).reshape(128, 24)
    cols[:, off["fn"]:off["fn"] + 8] = fm(inp["final_norm"])
    return cols


def host_rows(inp, depth):
    rows = np.zeros((depth, 3, 32), np.float32)
    for l in range(depth):
        rows[l, 0] = np.asarray(inp["ssd_dt_bias"][l], np.float32).reshape(32)
        rows[l, 1] = np.asarray(inp["ssd_a_log"][l], np.float32).reshape(32)
        rows[l, 2, :16] = np.asarray(inp["ssd_d"][l], np.float32)
    return rows


_W_KEYS = ["ffn1_w_gate", "ffn1_w_up", "ffn1_w_down", "ffn2_w_gate", "ffn2_w_up", "ffn2_w_down", "w_in", "ssd_w_out",
           "mla_w_uq", "mla_w_uk", "mla_w_uv", "mla_w_out", "pool_w", "pool_w_out", "sconv_w_out", "w_o"]


def make_in_maps(inp, depth, nb, n_cores):
    consts, rope, rot, invc = host_consts()
    cols = host_cols(inp, depth)
    rows = host_rows(inp, depth)
    shared = {k: np.ascontiguousarray(np.asarray(inp[k], np.float32)[:depth]) for k in _W_KEYS}
    shared["w_ada"] = np.ascontiguousarray(np.asarray(inp["w_ada"], np.float32)[:depth])
    shared.update({"cols": cols, "rows": rows, "consts": consts, "rope": rope, "rotm": rot, "invcnt": invc})
    x = np.asarray(inp["x"], np.float32)
    ctx = np.asarray(inp["ctx"], np.float32)
    c = np.asarray(inp["c"], np.float32)
    c_ctx = np.asarray(inp["c_ctx"], np.float32)
    maps = []
    for core in range(n_cores):
        b0 = core * nb
        cc = np.stack([c[b0 + (i % nb)] for i in range(2)] + [c_ctx], axis=0)
        cT = np.ascontiguousarray(cc.T.reshape(8, 128, 3).transpose(1, 0, 2))
        m = dict(shared)
        m["x"] = np.ascontiguousarray(x[b0:b0 + nb])
        m["ctx"] = np.ascontiguousarray(ctx[b0:b0 + nb])
        m["cT"] = cT
        maps.append(m)
    return maps


_CACHE = {}


def kernel(**inputs):
    n_cores, nb, depth = 8, 2, 4
    if "nc" not in _CACHE:
        _CACHE["nc"] = build(depth, nb)[0]
    nc = _CACHE["nc"]
    maps = make_in_maps(inputs, depth, nb, n_cores)
    res = run_bass_kernel_spmd(nc, maps, core_ids=list(range(n_cores)))
    out = np.concatenate([np.asarray(r["out"], np.float32) for r in res.results], axis=0)
    return out
```

```python
import numpy as np
import concourse.bass as bass
import concourse.mybir as mybir
from concourse.bass_utils import run_bass_kernel_spmd
from contextlib import ExitStack

F32 = mybir.dt.float32
BF16 = mybir.dt.bfloat16
F32R = mybir.dt.float32r
AF = mybir.ActivationFunctionType
ALU = mybir.AluOpType

SAME_ENGINE_SYNC = True
N_DMA_SEMS = 12
MLA_PARTS = 3
FAST_RECIP = False
MLA_SKIP = ""
ROPE_Q = "dve"
QB = 7
QOFF = 128
ROPE_CUT = 0
NHEADS_DBG = 8

D = 1024
T = 2304
NCTX = 256
NLAT = 2048
DFF = 2816
EPS = 1e-6
TILES = [(0, 256), (256, 768), (768, 1280), (1280, 1792), (1792, 2304)]
NCH = 18
O_Z, O_XBC, O_DT, O_QD, O_KVD, O_KR, O_PV, O_CV, O_G = 0, 1024, 3072, 3104, 3488, 3744, 3808, 4832, 7904
D_IN = 12000


class Op:
    __slots__ = ("q", "fn", "reads", "writes", "is_dma", "deps", "signal", "sig_idx",
                 "dma_sem", "dma_target", "dma_prev", "idx")


class Sched:
    QUEUES = ("pe", "act", "dve", "pool", "sp")

    def __init__(self, nc, es):
        self.nc = nc
        self.esem = {q: es.enter_context(nc.semaphore("s_" + q)) for q in ("pe", "act", "dve", "pool")}
        self.dsem = {q: [es.enter_context(nc.semaphore("d_%s%d" % (q, i))) for i in range(N_DMA_SEMS)]
                     for q in ("sp", "pool", "act")}
        self.cnt = {q: 0 for q in self.esem}
        self.dcount = {q: [0] * N_DMA_SEMS for q in self.dsem}
        self.drr = {q: 0 for q in self.dsem}
        self.waited_e = {q: {e: 0 for e in self.esem} for q in self.QUEUES}
        self.waited_d = {q: {} for q in self.QUEUES}
        self.n_instr = 0
        self.phase_names = []
        self.reset()

    def reset(self):
        self.ops = []
        self.last_write = {}
        self.readers = {}

    def add(self, q, fn, reads=(), writes=(), is_dma=False):
        op = Op()
        op.q, op.fn, op.reads, op.writes, op.is_dma = q, fn, tuple(reads), tuple(writes), is_dma
        op.signal, op.sig_idx, op.dma_sem, op.dma_target, op.dma_prev = False, 0, None, 0, 0
        op.idx = len(self.ops)
        deps = set()
        lw, rd = self.last_write, self.readers
        for k in op.reads:
            w = lw.get(k)
            if w is not None:
                deps.add(w)
        for k in op.writes:
            w = lw.get(k)
            if w is not None:
                deps.add(w)
            r = rd.get(k)
            if r:
                deps.update(r.values())
        deps.discard(op.idx)
        newest = {}
        keep = []
        for d in deps:
            dop = self.ops[d]
            if dop.is_dma:
                keep.append(d)
            elif newest.get(dop.q, -1) < d:
                newest[dop.q] = d
        op.deps = sorted(keep + list(newest.values()))
        rkey = ("dma", op.idx) if is_dma else q
        for k in op.reads:
            rd.setdefault(k, {})[rkey] = op.idx
        for k in op.writes:
            lw[k] = op.idx
            rd[k] = {}
        self.ops.append(op)
        return op

    def _skip(self, dop, op):
        if dop.is_dma or op.is_dma or dop.q != op.q:
            return False
        return dop.q in ("pe", "pool") or not SAME_ENGINE_SYNC

    def emit(self, name="?"):
        nc, ops = self.nc, self.ops
        if not ops:
            return
        self.phase_names.append(name)
        for op in ops:
            for d in op.deps:
                dop = ops[d]
                if dop.is_dma or self._skip(dop, op):
                    continue
                dop.signal = True
        last = {}
        for op in ops:
            if not op.is_dma:
                last[op.q] = op
        for op in last.values():
            op.signal = True
        for op in ops:
            if op.is_dma:
                i = self.drr[op.q] % N_DMA_SEMS
                self.drr[op.q] += 1
                op.dma_sem = (op.q, i)
                op.dma_prev = self.dcount[op.q][i]
                self.dcount[op.q][i] += 16
                op.dma_target = self.dcount[op.q][i]
            elif op.signal:
                self.cnt[op.q] += 1
                op.sig_idx = self.cnt[op.q]
        per_q = {q: [] for q in self.QUEUES}
        for op in ops:
            per_q[op.q].append(op)
        self.n_instr += len(ops)
        esem, dsem = self.esem, self.dsem
        final_e = dict(self.cnt)
        final_d = {q: list(v) for q, v in self.dcount.items()}

        def make(qname):
            def body(eng):
                if TRACE is not None:
                    eng = _TraceEng(eng, qname, TRACE)
                we = self.waited_e[qname]
                wd = self.waited_d[qname]

                def wait_for(dop):
                    if dop.is_dma:
                        key = dop.dma_sem
                        if wd.get(key, 0) < dop.dma_target:
                            eng.wait_ge(dsem[key[0]][key[1]], dop.dma_target)
                            wd[key] = dop.dma_target
                    else:
                        if we[dop.q] < dop.sig_idx:
                            eng.wait_ge(esem[dop.q], dop.sig_idx)
                            we[dop.q] = dop.sig_idx

                for op in per_q[qname]:
                    for d in op.deps:
                        dop = ops[d]
                        if self._skip(dop, op):
                            continue
                        wait_for(dop)
                    if op.is_dma:
                        key = op.dma_sem
                        if op.dma_prev > 0 and wd.get(key, 0) < op.dma_prev:
                            eng.wait_ge(dsem[key[0]][key[1]], op.dma_prev)
                            wd[key] = op.dma_prev
                        op.fn(eng).then_inc(dsem[key[0]][key[1]], 16)
                    else:
                        ins = op.fn(eng)
                        if op.signal:
                            ins.then_inc(esem[qname], 1)
                if qname == "sp":
                    for e, v in final_e.items():
                        if we[e] < v:
                            eng.wait_ge(esem[e], v)
                            we[e] = v
                    for dq, lst in final_d.items():
                        for i, v in enumerate(lst):
                            if v > 0 and wd.get((dq, i), 0) < v:
                                eng.wait_ge(dsem[dq][i], v)
                                wd[(dq, i)] = v
            return body

        with nc.Block() as block:
            block.tensor(make("pe"))
            block.scalar(make("act"))
            block.vector(make("dve"))
            block.gpsimd(make("pool"))
            block.sync(make("sp"))
        self.reset()


TRACE = None


class _TraceIns:
    def __init__(self, ins, q, log):
        self.ins, self.q, self.log = ins, q, log

    def then_inc(self, sem, n):
        self.log.append((self.q, "inc", id(sem), n))
        return self.ins.then_inc(sem, n)


class _TraceEng:
    def __init__(self, eng, q, log):
        self._e, self._q, self._log = eng, q, log

    def wait_ge(self, sem, val):
        self._log.append((self._q, "wait", id(sem), val))
        return self._e.wait_ge(sem, val)

    def __getattr__(self, name):
        f = getattr(self._e, name)

        def wrap(*a, **k):
            self._log.append((self._q, "op", name, 0))
            return _TraceIns(f(*a, **k), self._q, self._log)
        return wrap


class G:
    pass


def cols_layout(depth):
    per = [("gam0", 8), ("gam1", 8), ("gam2", 8), ("bada", 72), ("scw", 64), ("scb", 16), ("ssdn", 8),
           ("qn", 3), ("kvn", 2), ("psc", 8), ("cvw", 24)]
    off, o = {}, 0
    for l in range(depth):
        for n, w in per:
            off[(n, l)] = o
            o += w
    off["fn"] = o
    o += 8
    return off, o


def build(depth=4, nb=2, stop_after=None, dbg_shape=None, dbg_dtype=None):
    nc = bass.Bass("TRN2", target_bir_lowering=False)
    g = G()
    g.nc, g.depth, g.nb = nc, depth, nb

    def din(name, shape, dt=F32):
        return nc.dram_tensor(name, list(shape), dt, kind="ExternalInput").ap()

    g.x = din("x", [nb, NLAT, D])
    g.ctx = din("ctx", [nb, NCTX, D])
    g.cT = din("cT", [128, 8, 3])
    g.w_ada = din("w_ada", [depth, D, 9 * D])
    wnames = {}
    for pre in ("ffn1", "ffn2"):
        wnames[pre + "_w_gate"] = [depth, D, DFF]
        wnames[pre + "_w_up"] = [depth, D, DFF]
        wnames[pre + "_w_down"] = [depth, DFF, D]
    wnames.update({"w_in": [depth, D, D_IN], "ssd_w_out": [depth, D, D], "mla_w_uq": [depth, 384, 1536],
                   "mla_w_uk": [depth, 256, 1024], "mla_w_uv": [depth, 256, 1024], "mla_w_out": [depth, D, D],
                   "pool_w": [depth, 4, 256, 256], "pool_w_out": [depth, D, D], "sconv_w_out": [depth, D, D],
                   "w_o": [depth, D, D]})
    g.w = {k: din(k, v) for k, v in wnames.items()}
    g.coff, ncols = cols_layout(depth)
    g.cols_d = din("cols", [128, ncols])
    g.rows_d = din("rows", [depth, 3, 32])
    g.consts_d = din("consts", [128, 6 * 128])
    g.rope_d = din("rope", [2, 64, T])
    g.rot_d = din("rotm", [64, 64])
    g.invc_d = din("invcnt", [4, T])
    g.out = nc.dram_tensor("out", [nb, NLAT, D], F32, kind="ExternalOutput").ap()
    g.hs = nc.dram_tensor("hs", [nb, 128, 8, T], F32, kind="Internal").ap()
    g.xbcT = nc.dram_tensor("xbcT", [16, 128, T], BF16, kind="Internal").ap()
    g.sbwd = nc.dram_tensor("sbwd", [NCH, 128, 1024], BF16, kind="Internal").ap()
    g.mg_d = nc.dram_tensor("mg_d", [8, 128, T], BF16, kind="Internal").ap()
    g.dbg = None
    if stop_after is not None:
        g.dbg = nc.dram_tensor("dbg", list(dbg_shape), dbg_dtype, kind="ExternalOutput").ap()

    with ExitStack() as es:
        S = Sched(nc, es)
        g.S = S

        g.uid = 0

        def sb(es_, name, shape, dt):
            g.uid += 1
            return es_.enter_context(nc.sbuf_tensor("%s_u%d" % (name, g.uid), list(shape), dt))
        g.sb = sb
        g.ps = [es.enter_context(nc.psum_tensor("ps%d" % i, [128, 512], F32)) for i in range(8)]
        g.cols = sb(es, "cols", [128, ncols], F32)
        g.consts = sb(es, "consts", [128, 6 * 128], F32)
        g.identb = sb(es, "identb", [128, 128], BF16)
        g.onesb = sb(es, "onesb", [128, 128], BF16)
        g.mod = sb(es, "mod", [128, depth, 72, 3], F32)
        g.Gm = sb(es, "Gmod", [128, depth, 3, 8, 3], F32)
        g.Gt = sb(es, "Gate", [128, depth, 3, 8, 3], F32)
        g.aT = sb(es, "aT", [128, 8, T], BF16)
        g.identf = g.consts[:, 0:128]
        g.onesf = g.consts[:, 128:256]
        g.Lf = g.consts[:, 256:384]
        g.Ub = g.consts[:, 384:512]
        g.SL = g.consts[:, 512:640]
        g.SU = g.consts[:, 640:768]

        phases = Phases(g)
        phases.run(stop_after)
    g.n_instr = S.n_instr
    g.phase_names = S.phase_names
    return nc, g


def col(g, name, l, j):
    o = g.coff[(name, l)] + j
    return g.cols[:, o:o + 1]


def tile_of(kt):
    t = kt * 128
    for ti, (t0, t1) in enumerate(TILES):
        if t0 <= t < t1:
            return ti


def cls_of(t0, bl):
    return 2 if t0 < NCTX else bl


class Phases:
    def __init__(self, g):
        self.g = g
        self.S = g.S

    def DMA(self, q, out, in_, r, w):
        return self.S.add(q, lambda e: e.dma_start(out=out, in_=in_), r, w, True)

    def MM(self, out, lhsT, rhs, st, sp, r, w):
        return self.S.add("pe", lambda e: e.matmul(out, lhsT=lhsT, rhs=rhs, start=st, stop=sp), r, w)

    def TR(self, out, in_, ident, r, w):
        return self.S.add("pe", lambda e: e.transpose(out=out, in_=in_, identity=ident), r, w)

    def ACT(self, out, in_, func, r, w, bias=None, scale=None, accum_out=None):
        kw = {}
        if bias is not None:
            kw["bias"] = bias
        if scale is not None:
            kw["scale"] = scale
        if accum_out is not None:
            kw["accum_out"] = accum_out
        return self.S.add("act", lambda e: e.activation(out=out, in_=in_, func=func, **kw), r, w)

    def TT(self, q, out, in0, in1, op, r, w):
        return self.S.add(q, lambda e: e.tensor_tensor(out=out, in0=in0, in1=in1, op=op), r, w)

    def TS(self, q, out, in0, s1, op0, r, w, s2=None, op1=None):
        if op1 is None:
            return self.S.add(q, lambda e: e.tensor_scalar(out=out, in0=in0, scalar1=s1, scalar2=None, op0=op0), r, w)
        return self.S.add(q, lambda e: e.tensor_scalar(out=out, in0=in0, scalar1=s1, scalar2=s2, op0=op0, op1=op1), r, w)

    def STT(self, out, in0, scalar, in1, op0, op1, r, w):
        return self.S.add("dve", lambda e: e.scalar_tensor_tensor(out=out, in0=in0, scalar=scalar, in1=in1,
                                                                  op0=op0, op1=op1), r, w)

    def CP(self, q, out, in_, r, w):
        if q == "act":
            return self.ACT(out, in_, AF.Copy, r, w)
        return self.S.add(q, lambda e: e.tensor_copy(out=out, in_=in_), r, w)

    def RECIP(self, out, in_, r, w):
        if FAST_RECIP:
            return self.S.add("dve", lambda e: e.reciprocal_approx_fast(out=out, in_=in_), r, w)
        return self.S.add("dve", lambda e: e.reciprocal(out=out, in_=in_), r, w)

    def MEMSET(self, q, ap, val, w):
        return self.S.add(q, lambda e: e.memset(ap, val), (), w)

    def wslab(self, dst, src2d, key, p=128):
        return self.DMA("pool", dst, src2d.rearrange("(k p) n -> p k n", p=p), [], [key])

    def done(self, name, stop_after, dump=None):
        self.S.emit("done.1")
        if stop_after == name:
            g = self.g
            if dump is not None:
                src, q = dump
                self.DMA(q, g.dbg, src, [], [])
                self.S.emit("done.2")
            return True
        return False

    def run(self, stop_after):
        g = self.g
        self.phase_setup()
        if self.done("setup", stop_after, (g.hs[0], "sp")):
            return
        self.phase_ada()
        if self.done("ada", stop_after, (g.mod[:, 0], "sp")):
            return
        for l in range(g.depth):
            for b in range(g.nb):
                tag = "" if (l == 0 and b == 0) else "_%d_%d" % (l, b)
                self.phase_norm(l, b, 0)
                if self.done("norm0" + tag, stop_after, (g.aT[:], "sp")):
                    return
                self.phase_ffn(l, b, 0)
                if self.done("ffn1" + tag, stop_after, (g.hs[b], "sp")):
                    return
                self.phase_norm(l, b, 1)
                if self.done("norm1" + tag, stop_after, (g.aT[:], "sp")):
                    return
                with ExitStack() as mes:
                    g.yX = g.sb(mes, "yX", [128, 8, T], BF16)
                    self.phase_sconv(l, b)
                    if self.done("sconv" + tag, stop_after, (g.yX[:], "sp")):
                        return
                    self.phase_branch_out(l, b, g.w["sconv_w_out"][l], 3, True)
                    if self.done("sconv_out" + tag, stop_after, (g.mg_d.rearrange("k p t -> p k t"), "sp")):
                        return
                    self.phase_pool(l, b)
                    if self.done("pool" + tag, stop_after, (g.yX[:], "sp")):
                        return
                    self.phase_branch_out(l, b, g.w["pool_w_out"][l], 2, False)
                    if self.done("pool_out" + tag, stop_after, (g.mg_d.rearrange("k p t -> p k t"), "sp")):
                        return
                    self.phase_mla(l, b)
                    if self.done("mla" + tag, stop_after, (g.yX[:], "sp")):
                        return
                    self.phase_branch_out(l, b, g.w["mla_w_out"][l], 1, False)
                    if self.done("mla_out" + tag, stop_after, (g.mg_d.rearrange("k p t -> p k t"), "sp")):
                        return
                    if self.phase_ssd(l, b, stop_after, tag):
                        return
                    self.phase_ssd_post(l, b)
                    if self.done("ssd" + tag, stop_after, (g.yX[:], "sp")):
                        return
                    self.phase_branch_out(l, b, g.w["ssd_w_out"][l], 0, False)
                    if self.done("ssd_out" + tag, stop_after, (g.mg_d.rearrange("k p t -> p k t"), "sp")):
                        return
                    self.phase_wo(l, b)
                    if self.done("wo" + tag, stop_after, (g.hs[b], "sp")):
                        return
                self.phase_norm(l, b, 2)
                self.phase_ffn(l, b, 2)
                if self.done("ffn2" + tag, stop_after, (g.hs[b], "sp")):
                    return
        self.phase_final()
        self.S.emit("run.1")

    def phase_setup(self):
        g = self.g
        self.DMA("sp", g.cols[:], g.cols_d, [], ["cols"])
        self.DMA("sp", g.consts[:], g.consts_d, [], ["consts"])
        self.CP("dve", g.identb[:], g.identf, ["consts"], ["identb"])
        self.CP("dve", g.onesb[:], g.onesf, ["consts"], ["onesb"])
        with ExitStack() as es:
            xin = [g.sb(es, "xin%d" % i, [128, D], F32) for i in range(2)]
            xo = [g.sb(es, "xo%d" % i, [128, 8, 128], F32) for i in range(2)]
            n = 0
            for b in range(g.nb):
                for tt in range(NCH):
                    i = n % 2
                    n += 1
                    src = g.ctx[b, tt * 128:(tt + 1) * 128, :] if tt < 2 else g.x[b, (tt - 2) * 128:(tt - 1) * 128, :]
                    self.DMA("sp", xin[i][:], src, [], ["xin%d" % i])
                    for k in range(8):
                        bk = i * 2 + k // 4
                        self.TR(g.ps[bk][:, (k % 4) * 128:(k % 4 + 1) * 128], xin[i][:, k * 128:(k + 1) * 128],
                                g.identf, ["xin%d" % i, "consts"], ["ps%d" % bk])
                    self.ACT(xo[i][:, 0:4, :], g.ps[i * 2][:].rearrange("p (a b) -> p a b", a=4), AF.Copy,
                             ["ps%d" % (i * 2)], [("xo", i, 0)])
                    self.CP("dve", xo[i][:, 4:8, :], g.ps[i * 2 + 1][:].rearrange("p (a b) -> p a b", a=4),
                            ["ps%d" % (i * 2 + 1)], [("xo", i, 1)])
                    self.DMA("pool", g.hs[b][:, :, tt * 128:(tt + 1) * 128], xo[i][:], [("xo", i, 0), ("xo", i, 1)],
                             [("hs", b, tt)])
            self.S.emit("setup.1")

    def phase_ada(self):
        g = self.g
        with ExitStack() as es:
            ct = g.sb(es, "ct", [128, 8, 3], F32)
            sc = g.sb(es, "sc", [128, 8, 3], F32)
            wa = [g.sb(es, "wa%d" % i, [128, 8, 512], F32) for i in range(2)]
            self.DMA("sp", ct[:], g.cT, [], ["ct"])
            self.ACT(sc[:], ct[:], AF.Silu, ["ct"], ["sc"])
            n = 0
            for l in range(g.depth):
                for j in range(18):
                    i = n % 2
                    n += 1
                    self.DMA("sp", wa[i][:], g.w_ada[l][:, j * 512:(j + 1) * 512].rearrange("(k p) n -> p k n", p=128),
                             [], ["wa%d" % i])
                    for oc in range(4):
                        for k in range(8):
                            self.MM(g.ps[i][:, oc * 3:(oc + 1) * 3], wa[i][:, k, oc * 128:(oc + 1) * 128], sc[:, k, :],
                                    k == 0, k == 7, ["wa%d" % i, "sc"], ["ps%d" % i])
                    o = g.coff[("bada", l)] + j * 4
                    self.TT("dve", g.mod[:, l, j * 4:(j + 1) * 4, :],
                            g.ps[i][:, 0:12].rearrange("p (a b) -> p a b", a=4),
                            g.cols[:, o:o + 4].unsqueeze(2).broadcast_to([128, 4, 3]), ALU.add,
                            ["ps%d" % i, "cols"], [("mod", l, j)])
                for n_ in range(3):
                    o = g.coff[("gam%d" % n_, l)]
                    rk = [("mod", l, j) for j in range(18)]
                    self.STT(g.Gm[:, l, n_], g.mod[:, l, (3 * n_ + 1) * 8:(3 * n_ + 2) * 8, :], 1.0,
                             g.cols[:, o:o + 8].unsqueeze(2).broadcast_to([128, 8, 3]), ALU.add, ALU.mult,
                             rk + ["cols"], [("Gm", l, n_)])
                    self.TS("dve", g.Gt[:, l, n_], g.mod[:, l, (3 * n_ + 2) * 8:(3 * n_ + 3) * 8, :],
                            1.0 if n_ == 1 else 0.5, ALU.mult, rk, [("Gt", l, n_)])
            self.S.emit("ada.1")

    def phase_norm(self, l, b, n):
        g = self.g
        with ExitStack() as es:
            ht = g.sb(es, "ht", [128, 8, T], F32)
            sq = [g.sb(es, "sq%d" % i, [128, 8, 512], BF16) for i in range(2)]
            rs = [g.sb(es, "rs%d" % i, [128, 512], F32) for i in range(5)]
            tmp = [g.sb(es, "tmp%d" % i, [128, 8, 512], F32) for i in range(2)]
            for ti, (t0, t1) in enumerate(TILES):
                w = t1 - t0
                i = ti % 2
                self.DMA("sp", ht[:, :, t0:t1], g.hs[b][:, :, t0:t1], [], [("ht", ti)])
                self.ACT(sq[i][:, :, :w], ht[:, :, t0:t1], AF.Square, [("ht", ti)], ["sq%d" % i])
                for k in range(8):
                    self.MM(g.ps[i][:, :w], g.onesb[:], sq[i][:, k, :w], k == 0, k == 7, ["sq%d" % i], ["ps%d" % i])
                self.ACT(rs[ti][:, :w], g.ps[i][:, :w], AF.Sqrt, ["ps%d" % i], ["rs%d" % ti], bias=EPS, scale=1.0 / D)
                self.RECIP(rs[ti][:, :w], rs[ti][:, :w], ["rs%d" % ti], ["rs%d" % ti])
            for ti, (t0, t1) in enumerate(TILES):
                w = t1 - t0
                i = ti % 2
                cl = cls_of(t0, b)
                self.TT("dve", tmp[i][:, :, :w], ht[:, :, t0:t1], rs[ti][:, :w].unsqueeze(1).broadcast_to([128, 8, w]),
                        ALU.mult, [("ht", ti), "rs%d" % ti], ["tmp%d" % i])
                for k in range(8):
                    if k % 3 == 2:
                        self.TS("pool", g.aT[:, k, t0:t1], tmp[i][:, k, :w], g.Gm[:, l, n, k, cl:cl + 1], ALU.mult,
                                ["tmp%d" % i], [("aT", k, ti)], s2=g.mod[:, l, 3 * n * 8 + k, cl:cl + 1], op1=ALU.add)
                    else:
                        self.ACT(g.aT[:, k, t0:t1], tmp[i][:, k, :w], AF.Identity, ["tmp%d" % i], [("aT", k, ti)],
                                 scale=g.Gm[:, l, n, k, cl:cl + 1], bias=g.mod[:, l, 3 * n * 8 + k, cl:cl + 1])
            self.S.emit("norm.1")

    def phase_ffn(self, l, b, n):
        g = self.g
        pre = "ffn1" if n == 0 else "ffn2"
        wg_d, wu_d, wd_d = g.w[pre + "_w_gate"][l], g.w[pre + "_w_up"][l], g.w[pre + "_w_down"][l]
        aTk = [("aT", k, ti) for k in range(8) for ti in range(5)]
        with ExitStack() as es:
            hT = g.sb(es, "hT", [128, 22, T], BF16)
            with ExitStack() as es2:
                wg = [g.sb(es2, "wg%d" % i, [128, 8, 256], BF16) for i in range(2)]
                wu = [g.sb(es2, "wu%d" % i, [128, 8, 256], BF16) for i in range(2)]
                sg = [g.sb(es2, "sg%d" % i, [128, 512], F32) for i in range(2)]
                n2 = 0
                for fs in range(11):
                    i = fs % 2
                    self.wslab(wg[i][:], wg_d[:, fs * 256:(fs + 1) * 256], "wg%d" % i)
                    self.wslab(wu[i][:], wu_d[:, fs * 256:(fs + 1) * 256], "wu%d" % i)
                    for fo in range(2):
                        f = fs * 2 + fo
                        for ti, (t0, t1) in enumerate(TILES):
                            w = t1 - t0
                            j = n2 % 2
                            n2 += 1
                            pg, pu = g.ps[j * 2], g.ps[j * 2 + 1]
                            for k in range(8):
                                self.MM(pg[:, :w], wg[i][:, k, fo * 128:(fo + 1) * 128], g.aT[:, k, t0:t1], k == 0, k == 7,
                                        ["wg%d" % i], ["ps%d" % (j * 2)])
                            for k in range(8):
                                self.MM(pu[:, :w], wu[i][:, k, fo * 128:(fo + 1) * 128], g.aT[:, k, t0:t1], k == 0, k == 7,
                                        ["wu%d" % i], ["ps%d" % (j * 2 + 1)])
                            self.ACT(sg[j][:, :w], pg[:, :w], AF.Silu, ["ps%d" % (j * 2)], ["sg%d" % j])
                            self.TT("dve", hT[:, f, t0:t1], sg[j][:, :w], pu[:, :w], ALU.mult,
                                    ["sg%d" % j, "ps%d" % (j * 2 + 1)], [("hT", f, ti)])
                wd = [g.sb(es2, "wd%d" % i, [128, 22, 128], BF16) for i in range(2)]
                hc = [g.sb(es2, "hc%d" % i, [128, T], F32) for i in range(2)]
                n2 = 0
                for d in range(8):
                    i = d % 2
                    self.wslab(wd[i][:], wd_d[:, d * 128:(d + 1) * 128], "wd%d" % i)
                    self.DMA("sp", hc[i][:], g.hs[b][:, d, :], [], ["hc%d" % i])
                    for ti, (t0, t1) in enumerate(TILES):
                        w = t1 - t0
                        cl = cls_of(t0, b)
                        j = 4 + n2 % 2
                        n2 += 1
                        for f in range(22):
                            self.MM(g.ps[j][:, :w], wd[i][:, f, :], hT[:, f, t0:t1], f == 0, f == 21,
                                    ["wd%d" % i, ("hT", f, ti)], ["ps%d" % j])
                        self.STT(hc[i][:, t0:t1], g.ps[j][:, :w], g.Gt[:, l, n, d, cl:cl + 1], hc[i][:, t0:t1],
                                 ALU.mult, ALU.add, ["ps%d" % j, "hc%d" % i], ["hc%d" % i])
                    self.DMA("act", g.hs[b][:, d, :], hc[i][:], ["hc%d" % i], [("hs", d)])
                self.S.emit("ffn.2")

    def phase_branch_out(self, l, b, w_d, br, first):
        g = self.g
        w_in = g.w["w_in"][l]
        with ExitStack() as es:
            wo = [g.sb(es, "wo%d" % i, [128, 8, 128], BF16) for i in range(2)]
            wgt = [g.sb(es, "wgt%d" % i, [128, 8, 128], BF16) for i in range(2)]
            sig = [g.sb(es, "sig%d" % i, [128, 512], F32) for i in range(2)]
            pr = [g.sb(es, "pr%d" % i, [128, 512], BF16) for i in range(2)]
            mgc = [g.sb(es, "mgc%d" % i, [128, T], BF16) for i in range(2)]
            n2 = 0
            for d in range(8):
                i = d % 2
                self.wslab(wo[i][:], w_d[:, d * 128:(d + 1) * 128], "wo%d" % i)
                gc = O_G + br * D + d * 128
                self.wslab(wgt[i][:], w_in[:, gc:gc + 128], "wgt%d" % i)
                if not first:
                    self.DMA("pool", mgc[i][:], g.mg_d[d], [], [("mgc", i, ti) for ti in range(5)])
                for ti, (t0, t1) in enumerate(TILES):
                    w = t1 - t0
                    j = n2 % 2
                    n2 += 1
                    pa, pb = g.ps[j * 2], g.ps[j * 2 + 1]
                    for k in range(8):
                        self.MM(pa[:, :w], wo[i][:, k, :], g.yX[:, k, t0:t1], k == 0, k == 7, ["wo%d" % i],
                                ["ps%d" % (j * 2)])
                    for k in range(8):
                        self.MM(pb[:, :w], wgt[i][:, k, :], g.aT[:, k, t0:t1], k == 0, k == 7, ["wgt%d" % i],
                                ["ps%d" % (j * 2 + 1)])
                    self.ACT(sig[j][:, :w], pb[:, :w], AF.Sigmoid, ["ps%d" % (j * 2 + 1)], ["sig%d" % j])
                    if first:
                        self.TT("dve", mgc[i][:, t0:t1], sig[j][:, :w], pa[:, :w], ALU.mult,
                                ["sig%d" % j, "ps%d" % (j * 2)], [("mgc", i, ti)])
                    else:
                        self.TT("dve", pr[j][:, :w], sig[j][:, :w], pa[:, :w], ALU.mult,
                                ["sig%d" % j, "ps%d" % (j * 2)], ["pr%d" % j])
                        self.TT("dve", mgc[i][:, t0:t1], mgc[i][:, t0:t1], pr[j][:, :w], ALU.add,
                                ["pr%d" % j, ("mgc", i, ti)], [("mgc", i, ti)])
                self.DMA("sp", g.mg_d[d], mgc[i][:], [("mgc", i, ti) for ti in range(5)], [("mg_d", d)])
            self.S.emit("branch_out.1")

    def phase_wo(self, l, b):
        g = self.g
        with ExitStack() as es:
            wo = [g.sb(es, "wo%d" % i, [128, 8, 128], BF16) for i in range(2)]
            hc = [g.sb(es, "hc%d" % i, [128, T], F32) for i in range(2)]
            mg = g.sb(es, "mgall", [128, 8, T], BF16)
            for k in range(8):
                self.DMA("sp", mg[:, k, :], g.mg_d[k], [], [("mg", k)])
            mgk = [("mg", k) for k in range(8)]
            n2 = 0
            for d in range(8):
                i = d % 2
                self.wslab(wo[i][:], g.w["w_o"][l][:, d * 128:(d + 1) * 128], "wo%d" % i)
                self.DMA("sp", hc[i][:], g.hs[b][:, d, :], [], ["hc%d" % i])
                for ti, (t0, t1) in enumerate(TILES):
                    w = t1 - t0
                    cl = cls_of(t0, b)
                    j = n2 % 2
                    n2 += 1
                    for k in range(8):
                        self.MM(g.ps[j][:, :w], wo[i][:, k, :], mg[:, k, t0:t1], k == 0, k == 7, ["wo%d" % i, ("mg", k)],
                                ["ps%d" % j])
                    self.STT(hc[i][:, t0:t1], g.ps[j][:, :w], g.Gt[:, l, 1, d, cl:cl + 1], hc[i][:, t0:t1],
                             ALU.mult, ALU.add, ["ps%d" % j, "hc%d" % i], ["hc%d" % i])
                self.DMA("act", g.hs[b][:, d, :], hc[i][:], ["hc%d" % i], [("hs", d)])
            self.S.emit("wo.1")

    def phase_sconv(self, l, b):
        g = self.g
        w_in = g.w["w_in"][l]
        with ExitStack() as es:
            wc = [g.sb(es, "wc%d" % i, [128, 8, 384], BF16) for i in range(2)]
            u = [g.sb(es, "u%d" % i, [128, T], F32) for i in range(2)]
            gb = [g.sb(es, "gb%d" % i, [128, T], F32) for i in range(2)]
            yv = [g.sb(es, "yv%d" % i, [128, T], F32) for i in range(2)]
            ctm = [g.sb(es, "ctm%d" % i, [128, 512], F32) for i in range(2)]
            n2 = 0
            for jc in range(8):
                i = jc % 2
                for s in range(3):
                    c0 = O_CV + s * D + jc * 128
                    self.wslab(wc[i][:, :, s * 128:(s + 1) * 128], w_in[:, c0:c0 + 128], ("wc", i, s))
                for ti, (t0, t1) in enumerate(TILES):
                    w = t1 - t0
                    j = n2 % 2
                    n2 += 1
                    pbk = [g.ps[j * 3 + s] for s in range(3)]
                    for s in range(3):
                        for k in range(8):
                            self.MM(pbk[s][:, :w], wc[i][:, k, s * 128:(s + 1) * 128], g.aT[:, k, t0:t1], k == 0, k == 7,
                                    [("wc", i, s)], ["ps%d" % (j * 3 + s)])
                    self.ACT(gb[i][:, t0:t1], pbk[0][:, :w], AF.Copy, ["ps%d" % (j * 3)], [("gb", i, ti)])
                    self.ACT(ctm[j][:, :w], pbk[1][:, :w], AF.Copy, ["ps%d" % (j * 3 + 1)], ["ctm%d" % j])
                    self.TT("dve", u[i][:, t0:t1], ctm[j][:, :w], pbk[2][:, :w], ALU.mult,
                            ["ctm%d" % j, "ps%d" % (j * 3 + 2)], [("u", i, ti)])
                uk = [("u", i, ti) for ti in range(5)]
                o = g.coff[("cvw", l)] + jc * 3
                w0, w1, w2 = (g.cols[:, o + t:o + t + 1] for t in range(3))
                for (s0, L) in ((0, NCTX), (NCTX, NLAT)):
                    yk = ("yv", i, s0)
                    self.TS("dve", yv[i][:, s0:s0 + L], u[i][:, s0:s0 + L], w1, ALU.mult, uk, [yk])
                    self.STT(yv[i][:, s0 + 1:s0 + L], u[i][:, s0:s0 + L - 1], w0, yv[i][:, s0 + 1:s0 + L],
                             ALU.mult, ALU.add, uk + [yk], [yk])
                    self.STT(yv[i][:, s0:s0 + L - 1], u[i][:, s0 + 1:s0 + L], w2, yv[i][:, s0:s0 + L - 1],
                             ALU.mult, ALU.add, uk + [yk], [yk])
                self.TT("dve", g.yX[:, jc, :], gb[i][:], yv[i][:], ALU.mult,
                        [("gb", i, ti) for ti in range(5)] + [("yv", i, 0), ("yv", i, NCTX)], [("yX", jc)])
            self.S.emit("sconv.1")

    def phase_pool(self, l, b):
        g = self.g
        w_in = g.w["w_in"][l]
        PADT = T + 32
        SEQ = ((8, 0, NCTX), (8 + NCTX + 16, NCTX, NLAT))

        def poff(t0):
            return 8 + t0 if t0 < NCTX else 8 + NCTX + 16 + (t0 - NCTX)
        with ExitStack() as es:
            wp = [g.sb(es, "wp%d" % i, [128, 8, 256], BF16) for i in range(2)]
            pw = [g.sb(es, "pw%d" % i, [128, 2, 256], BF16) for i in range(2)]
            ic = [g.sb(es, "ic%d" % i, [128, T], F32) for i in range(2)]
            vp = [g.sb(es, "vp%d" % i, [128, 2, PADT], F32) for i in range(2)]
            s1 = g.sb(es, "s1", [128, 2, PADT], F32)
            s2 = g.sb(es, "s2", [128, 2, PADT], F32)
            pT = [g.sb(es, "pT%d" % i, [128, 2, T], BF16) for i in range(2)]
            for i_ in range(2):
                self.MEMSET("pool", vp[i_][:], 0.0, ["vp%d" % i_])
            n2 = [0]

            def proj(gi):
                i = gi % 2
                c0 = O_PV + gi * 256
                self.wslab(wp[i][:], w_in[:, c0:c0 + 256], "wp%d" % i)
                self.wslab(pw[i][:], g.w["pool_w"][l, gi], "pw%d" % i)
                self.DMA("sp", ic[i][:], g.invc_d[gi:gi + 1, :].partition_broadcast(128), [], ["ic%d" % i])
                for c in range(2):
                    for ti, (t0, t1) in enumerate(TILES):
                        w = t1 - t0
                        j = n2[0] % 2
                        n2[0] += 1
                        for k in range(8):
                            self.MM(g.ps[j][:, :w], wp[i][:, k, c * 128:(c + 1) * 128], g.aT[:, k, t0:t1], k == 0, k == 7,
                                    ["wp%d" % i], ["ps%d" % j])
                        po = poff(t0)
                        self.ACT(vp[i][:, c, po:po + w], g.ps[j][:, :w], AF.Copy, ["ps%d" % j], ["vp%d" % i])

            def window(gi):
                i = gi % 2
                win = (2, 4, 8, 16)[gi]
                left = win // 2
                src, srck = vp[i], "vp%d" % i
                bufs = [(s1, "s1"), (s2, "s2")]
                step = 1
                bi = 0
                valid = PADT
                while step < win:
                    dst, dstk = bufs[bi]
                    bi ^= 1
                    n_ = valid - step
                    self.TT("dve", dst[:, :, 0:n_], src[:, :, 0:n_], src[:, :, step:step + n_], ALU.add, [srck], [dstk])
                    src, srck = dst, dstk
                    valid = n_
                    step *= 2
                oth, othk = bufs[bi]
                for c in range(2):
                    for (po, s0, L) in SEQ:
                        self.TT("dve", oth[:, c, :L], src[:, c, po - left:po - left + L], ic[i][:, s0:s0 + L], ALU.mult,
                                [srck, "ic%d" % i], [othk])
                        self.TT("dve", pT[i][:, c, s0:s0 + L], oth[:, c, :L], vp[i][:, c, po:po + L], ALU.subtract,
                                [othk, "vp%d" % i], ["pT%d" % i])

            def outp(gi):
                i = gi % 2
                for oc in range(2):
                    o = g.coff[("psc", l)] + gi * 2 + oc
                    for ti, (t0, t1) in enumerate(TILES):
                        w = t1 - t0
                        j = 4 + n2[0] % 2
                        n2[0] += 1
                        for c in range(2):
                            self.MM(g.ps[j][:, :w], pw[i][:, c, oc * 128:(oc + 1) * 128], pT[i][:, c, t0:t1], c == 0, c == 1,
                                    ["pw%d" % i, "pT%d" % i], ["ps%d" % j])
                        self.ACT(g.yX[:, gi * 2 + oc, t0:t1], g.ps[j][:, :w], AF.Identity, ["ps%d" % j],
                                 [("yX", gi * 2 + oc, ti)], scale=g.cols[:, o:o + 1])
            proj(0)
            for gi in range(4):
                if gi + 1 < 4:
                    proj(gi + 1)
                window(gi)
                outp(gi)
            self.S.emit("pool.1")

    def phase_mla(self, l, b):
        g = self.g
        w_in = g.w["w_in"][l]
        aTk = []
        with ExitStack() as es:
            wuq = g.sb(es, "wuq", [128, 3, 1536], BF16)
            wuk = g.sb(es, "wuk", [128, 2, 1024], BF16)
            wuv = g.sb(es, "wuv", [128, 2, 1024], BF16)
            rope = g.sb(es, "rope", [64, 2, T], BF16)
            rotf = g.sb(es, "rotf", [64, 64], F32)
            rotb = g.sb(es, "rotb", [64, 64], BF16)
            cq = g.sb(es, "cq", [128, 3, T], BF16)
            ckv = g.sb(es, "ckv", [128, 2, T], BF16)
            kpe = g.sb(es, "kpe", [128, T], BF16)
            rf = g.sb(es, "rf", [64, 512], F32)
            rb = g.sb(es, "rb", [64, 512], BF16)
            r1 = g.sb(es, "r1", [64, 512], F32)
            r2 = g.sb(es, "r2", [64, 512], F32)
            es1 = ExitStack()
            wq_in = g.sb(es1, "wq_in", [128, 8, 384], BF16)
            wkv_in = g.sb(es1, "wkv_in", [128, 8, 256], BF16)
            wkr_in = g.sb(es1, "wkr_in", [128, 8, 64], BF16)
            sqm = g.sb(es1, "sqm", [128, 3, 512], BF16)
            rsm = g.sb(es1, "rsm", [128, 512], F32)
            self.MEMSET("dve", kpe[64:128, :], 0.0, ["kpez"])
            self.wslab(wq_in[:], w_in[:, O_QD:O_QD + 384], "wq_in")
            self.wslab(wkv_in[:], w_in[:, O_KVD:O_KVD + 256], "wkv_in")
            self.wslab(wkr_in[:], w_in[:, O_KR:O_KR + 64], "wkr_in")
            self.wslab(wuq[:], g.w["mla_w_uq"][l], "wuq")
            self.wslab(wuk[:], g.w["mla_w_uk"][l], "wuk")
            self.wslab(wuv[:], g.w["mla_w_uv"][l], "wuv")
            self.DMA("pool", rope[:], g.rope_d.rearrange("a p t -> p a t"), [], ["rope"])
            self.DMA("sp", rotf[:], g.rot_d, [], ["rotf"])
            self.CP("dve", rotb[:], rotf[:], ["rotf"], ["rotb"])

            def rope_apply(src_ps, dst, t0, t1, pskey, psr, dkey, part1=False, stage=0):
                w = t1 - t0
                if stage in (0, 1):
                    self.ACT(rf[:, :w], src_ps[0:64, :w], AF.Copy, [pskey], ["rf"])
                    self.CP("dve", rb[:, :w], rf[:, :w], ["rf"], ["rb"])
                if stage in (0, 2):
                    self.MM(g.ps[psr][0:64, :w], rotb[:], rb[:, :w], True, True, ["rotb", "rb"], ["ps%d" % psr])
                    self.TT("dve", r1[:, :w], rf[:, :w], rope[:, 0, t0:t1], ALU.mult, ["rf", "rope"], ["r1"])
                    self.TT("dve", r2[:, :w], g.ps[psr][0:64, :w], rope[:, 1, t0:t1], ALU.mult, ["ps%d" % psr, "rope"], ["r2"])
                    self.TT(ROPE_Q, dst[0:64, t0:t1], r1[:, :w], r2[:, :w], ALU.add, ["r1", "r2"], [dkey])

            sqk = g.sb(es1, "sqk", [128, 2, 512], BF16)
            rsk = g.sb(es1, "rsk", [128, 512], F32)
            for ti, (t0, t1) in enumerate(TILES):
                w = t1 - t0
                for c in range(3):
                    for k in range(8):
                        self.MM(g.ps[c][:, :w], wq_in[:, k, c * 128:(c + 1) * 128], g.aT[:, k, t0:t1], k == 0, k == 7,
                                ["wq_in"], ["ps%d" % c])
                for c in range(2):
                    for k in range(8):
                        self.MM(g.ps[6 + c][:, :w], wkv_in[:, k, c * 128:(c + 1) * 128], g.aT[:, k, t0:t1], k == 0, k == 7,
                                ["wkv_in"], ["ps%d" % (6 + c)])
                for k in range(8):
                    self.MM(g.ps[4][0:64, :w], wkr_in[:, k, :], g.aT[:, k, t0:t1], k == 0, k == 7, ["wkr_in"], ["ps4"])
                for c in range(3):
                    self.ACT(sqm[:, c, :w], g.ps[c][:, :w], AF.Square, ["ps%d" % c], [("sqm", c)])
                for c in range(2):
                    self.ACT(sqk[:, c, :w], g.ps[6 + c][:, :w], AF.Square, ["ps%d" % (6 + c)], [("sqk", c)])
                rope_apply(g.ps[4], kpe, t0, t1, "ps4", 5, ("kpe", ti), part1=True, stage=1)
                for c in range(3):
                    self.MM(g.ps[3][:, :w], g.onesb[:], sqm[:, c, :w], c == 0, c == 2, [("sqm", c)], ["ps3"])
                self.ACT(rsm[:, :w], g.ps[3][:, :w], AF.Sqrt, ["ps3"], ["rsm"], bias=EPS, scale=1.0 / 384)
                self.RECIP(rsm[:, :w], rsm[:, :w], ["rsm"], ["rsm"])
                for c in range(2):
                    self.MM(g.ps[3][:, :w], g.onesb[:], sqk[:, c, :w], c == 0, c == 1, [("sqk", c)], ["ps3"])
                self.ACT(rsk[:, :w], g.ps[3][:, :w], AF.Sqrt, ["ps3"], ["rsk"], bias=EPS, scale=1.0 / 256)
                self.RECIP(rsk[:, :w], rsk[:, :w], ["rsk"], ["rsk"])
                for c in range(3):
                    self.STT(cq[:, c, t0:t1], g.ps[c][:, :w], col(g, "qn", l, c), rsm[:, :w], ALU.mult, ALU.mult,
                             ["ps%d" % c, "rsm"], [("cq", c, ti)])
                for c in range(2):
                    self.STT(ckv[:, c, t0:t1], g.ps[6 + c][:, :w], col(g, "kvn", l, c), rsk[:, :w], ALU.mult, ALU.mult,
                             ["ps%d" % (6 + c), "rsk"], [("ckv", c, ti)])
                rope_apply(g.ps[4], kpe, t0, t1, "ps4", 5, ("kpe", ti), part1=True, stage=2)
            self.S.emit("mla.1")
            es1.close()
            if MLA_PARTS == 1:
                return
            qn = [g.sb(es, "qn%d" % i, [128, T], BF16) for i in range(2)]
            qpe = [g.sb(es, "qpe%d" % i, [128, T], BF16) for i in range(2)]
            for i_ in range(2):
                self.MEMSET("dve", qpe[i_][64:128, :], 0.0, [("qpez", i_)])
            kn = [g.sb(es, "kn%d" % i, [128, T], BF16) for i in range(2)]
            Vh = [g.sb(es, "Vh%d" % i, [128, NCH, 128], BF16) for i in range(2)]
            PT = [g.sb(es, "PT%d" % i, [128, 512], BF16) for i in range(4)]
            rz = [g.sb(es, "rz%d" % i, [128, 512], F32) for i in range(2)]
            zsb = [g.sb(es, "zsb%d" % i, [128, 512], F32) for i in range(2)]
            osb = [g.sb(es, "osb%d" % i, [128, 512], F32) for i in range(2)]
            cqk = []
            ckvk = []
            kpek = []
            scale = 192.0 ** -0.5
            nacc = 0
            npt = 0
            for h in range(NHEADS_DBG):
                i = h % 2
                for ti, (t0, t1) in enumerate(TILES):
                    w = t1 - t0
                    for c in range(3):
                        self.MM(g.ps[QB][0:64, :w], wuq[:, c, h * 192 + QOFF:h * 192 + QOFF + 64], cq[:, c, t0:t1], c == 0, c == 2,
                                ["wuq"] + cqk, ["ps%d" % QB])
                    rope_apply(g.ps[QB], qpe[i], t0, t1, "ps%d" % QB, 5, ("qpe", i, ti), stage=1)
                    for c in range(3):
                        self.MM(g.ps[6][:, :w], wuq[:, c, h * 192:h * 192 + 128], cq[:, c, t0:t1], c == 0, c == 2,
                                ["wuq"] + cqk, ["ps6"])
                    self.ACT(qn[i][:, t0:t1], g.ps[6][:, :w], AF.Copy, ["ps6"], [("qn", i, ti)])
                    for c in range(2):
                        self.MM(g.ps[6][:, :w], wuk[:, c, h * 128:(h + 1) * 128], ckv[:, c, t0:t1], c == 0, c == 1,
                                ["wuk"] + ckvk, ["ps6"])
                    self.CP("dve", kn[i][:, t0:t1], g.ps[6][:, :w], ["ps6"], [("kn", i, ti)])
                    rope_apply(g.ps[QB], qpe[i], t0, t1, "ps%d" % QB, 5, ("qpe", i, ti), stage=2)
                for kq in range(0, NCH, 4):
                    if "d" in MLA_SKIP:
                        break
                    nk = min(4, NCH - kq)
                    for kk in range(nk):
                        kt = kq + kk
                        for c in range(2):
                            self.MM(g.ps[7][:, kk * 128:(kk + 1) * 128], ckv[:, c, kt * 128:(kt + 1) * 128],
                                    wuv[:, c, h * 128:(h + 1) * 128], c == 0, c == 1, ["wuv"] + ckvk, ["ps7"])
                    self.ACT(Vh[i][:, kq:kq + nk, :], g.ps[7][:, :nk * 128].rearrange("p (a b) -> p a b", a=nk), AF.Copy,
                             ["ps7"], [("Vh", i, kq)])
                for qi, (q0, q1) in enumerate(TILES):
                    if MLA_PARTS == 2:
                        break
                    w = q1 - q0
                    keys = [0, 1] if qi == 0 else list(range(NCH))
                    ja = nacc % 2
                    nacc += 1
                    pO, pZ = g.ps[2], g.ps[3]
                    kO, kZ = "ps2", "ps3"
                    PSB = (0, 1, 4)

                    def smm(kt, pj):
                        self.MM(g.ps[pj][:, :w], kn[i][:, kt * 128:(kt + 1) * 128], qn[i][:, q0:q1], True, False,
                                [("kn", i, tile_of(kt)), ("qn", i, qi)], ["ps%d" % pj])
                        self.MM(g.ps[pj][:, :w], kpe[:, kt * 128:(kt + 1) * 128], qpe[i][:, q0:q1], False, True,
                                kpek + [("qpe", i, qi)], ["ps%d" % pj])
                    m0 = npt
                    for la in range(min(2, len(keys))):
                        smm(keys[la], PSB[(m0 + la) % 3])
                    for n_, kt in enumerate(keys):
                        pj = PSB[npt % 3]
                        pt = npt % 4
                        npt += 1
                        if n_ + 2 < len(keys):
                            smm(keys[n_ + 2], PSB[(m0 + n_ + 2) % 3])
                        self.ACT(PT[pt][:, :w], g.ps[pj][:, :w], AF.Exp, ["ps%d" % pj], ["PT%d" % pt], scale=scale)
                        self.MM(pO[:, :w], Vh[i][:, kt, :], PT[pt][:, :w], n_ == 0, n_ == len(keys) - 1,
                                [("Vh", i, (kt // 4) * 4), "PT%d" % pt], [kO])
                        self.MM(pZ[:, :w], g.onesb[:], PT[pt][:, :w], n_ == 0, n_ == len(keys) - 1, ["PT%d" % pt], [kZ])
                    self.ACT(zsb[ja][:, :w], pZ[:, :w], AF.Copy, [kZ], ["zsb%d" % ja])
                    self.CP("dve", osb[ja][:, :w], pO[:, :w], [kO], ["osb%d" % ja])
                    self.RECIP(rz[ja][:, :w], zsb[ja][:, :w], ["zsb%d" % ja], ["rz%d" % ja])
                    self.TT("pool", g.yX[:, h, q0:q1], osb[ja][:, :w], rz[ja][:, :w], ALU.mult, ["osb%d" % ja, "rz%d" % ja],
                            [("yX", h, qi)])
            self.S.emit("mla.2")

    def phase_ssd(self, l, b, stop_after, tag):
        g = self.g
        w_in = g.w["w_in"][l]
        with ExitStack() as es:
            wx = [g.sb(es, "wx%d" % i, [128, 8, 128], BF16) for i in range(2)]
            raw = [g.sb(es, "raw%d" % i, [128, T], F32) for i in range(2)]
            yv = [g.sb(es, "yc%d" % i, [128, T], F32) for i in range(2)]
            ob = [g.sb(es, "ob%d" % i, [128, T], BF16) for i in range(2)]
            n2 = [0]

            def st1(cc):
                i = cc % 2
                c0 = O_XBC + cc * 128
                self.wslab(wx[i][:], w_in[:, c0:c0 + 128], "wx%d" % i)
                for ti, (t0, t1) in enumerate(TILES):
                    w = t1 - t0
                    j = n2[0] % 2
                    n2[0] += 1
                    for k in range(8):
                        self.MM(g.ps[j][:, :w], wx[i][:, k, :], g.aT[:, k, t0:t1], k == 0, k == 7, ["wx%d" % i], ["ps%d" % j])
                    self.ACT(raw[i][:, t0:t1], g.ps[j][:, :w], AF.Copy, ["ps%d" % j], [("raw", i, ti)])

            def st2(cc):
                i = cc % 2
                rk = [("raw", i, ti) for ti in range(5)]
                o = g.coff[("scw", l)] + cc * 4
                w0, w1, w2, w3 = (g.cols[:, o + t:o + t + 1] for t in range(4))
                for (s0, L) in ((0, NCTX), (NCTX, NLAT)):
                    yk = ("yc", i, s0)
                    self.TS("dve", yv[i][:, s0:s0 + L], raw[i][:, s0:s0 + L], w2, ALU.mult, rk, [yk])
                    self.STT(yv[i][:, s0 + 2:s0 + L], raw[i][:, s0:s0 + L - 2], w0, yv[i][:, s0 + 2:s0 + L],
                             ALU.mult, ALU.add, rk + [yk], [yk])
                    self.STT(yv[i][:, s0 + 1:s0 + L], raw[i][:, s0:s0 + L - 1], w1, yv[i][:, s0 + 1:s0 + L],
                             ALU.mult, ALU.add, rk + [yk], [yk])
                    self.STT(yv[i][:, s0:s0 + L - 1], raw[i][:, s0 + 1:s0 + L], w3, yv[i][:, s0:s0 + L - 1],
                             ALU.mult, ALU.add, rk + [yk], [yk])

            def st3(cc):
                i = cc % 2
                self.ACT(ob[i][:], yv[i][:], AF.Silu, [("yc", i, 0), ("yc", i, NCTX)], ["ob%d" % i],
                         bias=col(g, "scb", l, cc))
                self.DMA("sp", g.xbcT[cc], ob[i][:], ["ob%d" % i], [("xbcT", cc)])
            for cc in range(16):
                st1(cc)
                if cc >= 1:
                    st3(cc - 1)
                st2(cc)
            st3(15)
            self.S.emit("ssd.1")
        if self.done("ssd_conv" + tag, stop_after, (g.xbcT, "sp")):
            return True

        with ExitStack() as es:
            f = lambda name, shape, dt=F32: g.sb(es, name, shape, dt)
            wdt = f("wdt", [128, 8, 32], BF16)
            rowb = f("rowb", [128, 96])
            dt = f("dt", [128, NCH, 32])
            da = f("da", [128, NCH, 32])
            cs = f("cs", [128, NCH, 32])
            tot = f("tot", [128, NCH, 32])
            dte = f("dte", [128, NCH, 32])
            ecs = f("ecs", [128, NCH, 32])
            etot = f("etot", [128, NCH, 32])
            sdd = f("sdd", [128, NCH, 32])
            self.wslab(wdt[:], w_in[:, O_DT:O_DT + 32], "wdt")
            self.DMA("sp", rowb[:], g.rows_d[l:l + 1].rearrange("a r n -> a (r n)").partition_broadcast(128), [], ["rowb"])
            for (c0, c1, bk) in ((0, 16, 0), (16, 18, 1)):
                for c in range(c0, c1):
                    for k in range(8):
                        self.MM(g.ps[bk][:, (c - c0) * 32:(c - c0 + 1) * 32], g.aT[:, k, c * 128:(c + 1) * 128], wdt[:, k, :],
                                k == 0, k == 7, ["wdt"], ["ps%d" % bk])
                nn = c1 - c0
                self.TT("dve", dt[:, c0:c1, :], g.ps[bk][:, :nn * 32].rearrange("p (a b) -> p a b", a=nn),
                        rowb[:, 0:32].unsqueeze(1).broadcast_to([128, nn, 32]), ALU.add, ["ps%d" % bk, "rowb"], [("dt", bk)])
            dtk = [("dt", 0), ("dt", 1)]
            self.ACT(dt[:], dt[:], AF.Exp, dtk, ["dt"])
            self.ACT(dt[:], dt[:], AF.Ln, ["dt"], ["dt"], bias=1.0)
            self.ACT(rowb[:, 32:64], rowb[:, 32:64], AF.Exp, ["rowb"], ["rowa"])
            self.TS("dve", rowb[:, 32:64], rowb[:, 32:64], -1.0, ALU.mult, ["rowa"], ["rowa"])
            self.TT("dve", da[:], dt[:], rowb[:, 32:64].unsqueeze(1).broadcast_to([128, NCH, 32]), ALU.mult, ["dt", "rowa"], ["da"])
            for d in range(2):
                tri = g.Lf if d == 0 else g.Ub
                self.MM(g.ps[2 + d][:, :NCH * 16], tri, da[:, :, d * 16:(d + 1) * 16], True, True, ["da", "consts"],
                        ["ps%d" % (2 + d)])
                self.CP("dve", cs[:, :, d * 16:(d + 1) * 16], g.ps[2 + d][:, :NCH * 16].rearrange("p (a b) -> p a b", a=NCH),
                        ["ps%d" % (2 + d)], [("cs", d)])
                self.MM(g.ps[4 + d][:, :NCH * 16], g.onesf, da[:, :, d * 16:(d + 1) * 16], True, True, ["da", "consts"],
                        ["ps%d" % (4 + d)])
                self.CP("dve", tot[:, :, d * 16:(d + 1) * 16], g.ps[4 + d][:, :NCH * 16].rearrange("p (a b) -> p a b", a=NCH),
                        ["ps%d" % (4 + d)], [("tot", d)])
            csk = [("cs", 0), ("cs", 1)]
            totk = [("tot", 0), ("tot", 1)]
            self.TT("dve", dte[:], tot[:], cs[:], ALU.subtract, csk + totk, ["dte"])
            self.ACT(dte[:], dte[:], AF.Exp, ["dte"], ["dte"])
            self.ACT(ecs[:], cs[:], AF.Exp, csk, ["ecs"])
            self.ACT(etot[:], tot[:], AF.Exp, totk, ["etot"])
            self.TT("dve", sdd[:], dt[:], dte[:], ALU.mult, ["dt", "dte"], ["sdd"])
            self.S.emit("ssd.2")
            if stop_after == "ssd_dt" + tag:
                self.DMA("sp", g.dbg[0], dt[:], [], [])
                self.DMA("sp", g.dbg[1], cs[:], [], [])
                self.DMA("sp", g.dbg[2], tot[:], [], [])
                self.S.emit("ssd.3")
                return True

            def f2(name, shape, dt=F32):
                return [f("%s%d" % (name, i), shape, dt) for i in range(2)]
            xc = f2("xc", [128, 16, 128], BF16)
            xs_tok = f2("xs_tok", [128, 16, 64], BF16)
            B_tok = f2("B_tok", [128, 4, 128], BF16)
            xdd = f2("xdd", [128, 16, 64], BF16)
            xdt = f2("xdt", [128, 16, 64], BF16)
            Sst = f("Sst", [128, 16, 64])
            Sbf = f2("Sbf", [128, D], BF16)
            Sbw = f2("Sbw", [128, D], BF16)
            Gmf = f2("Gmf", [128, 4, 128])
            Gmb = f2("Gmb", [128, 4, 128])
            yacc = f2("yacc", [128, 16, 64])
            ytmp = f2("ytmp", [128, 16, 64])
            rseg = f2("rseg", [128, 4, 128])
            Eb = f2("Eb", [128, 4, 128])
            Mt = f2("Mt", [128, 4, 128], BF16)
            yzn = f2("yzn", [128, D], BF16)

            def load_a(c, i):
                self.DMA("sp", xc[i][:], g.xbcT[:, :, c * 128:(c + 1) * 128].rearrange("c p t -> p c t"), [], ["xc%d" % i])
                pxs = g.ps[0][:].bitcast(BF16)
                pbt = g.ps[2][:].bitcast(BF16)
                for k in range(8):
                    self.TR(pxs[:, k * 128:(k + 1) * 128], xc[i][:, k, :], g.identb[:], ["xc%d" % i, "identb"], ["ps0"])
                for gq in range(4):
                    self.TR(pbt[:, gq * 128:(gq + 1) * 128], xc[i][:, 8 + gq, :], g.identb[:], ["xc%d" % i, "identb"], ["ps2"])

            def load_b(c, i):
                pxs = g.ps[0][:].bitcast(BF16)
                pbt = g.ps[2][:].bitcast(BF16)
                self.CP("dve", xs_tok[i][:].rearrange("p a b -> p (a b)"), pxs[:, :], ["ps0"], ["xs_tok%d" % i])
                self.ACT(B_tok[i][:].rearrange("p a b -> p (a b)"), pbt[:, 0:512], AF.Copy, ["ps2"], ["B_tok%d" % i])

            def load_chunk(c, i):
                load_a(c, i)
                load_b(c, i)

            def local_mm(c, d, i, bp):
                self.TT("pool", xdd[i][:], xs_tok[i][:], sdd[:, c, d * 16:(d + 1) * 16].unsqueeze(2).broadcast_to([128, 16, 64]),
                        ALU.mult, ["xs_tok%d" % i, "sdd"], ["xdd%d" % i])
                for gq in range(4):
                    bk = bp + gq // 2
                    self.MM(g.ps[bk][:, (gq % 2) * 256:(gq % 2 + 1) * 256], B_tok[i][:, gq, :],
                            xdd[i][:, gq * 4:(gq + 1) * 4, :].rearrange("p a b -> p (a b)"), True, True,
                            ["B_tok%d" % i, "xdd%d" % i], ["ps%d" % bk])

            def state_apply(c, d, bp):
                self.TT("dve", Sst[:], Sst[:], etot[:, c, d * 16:(d + 1) * 16].unsqueeze(2).broadcast_to([128, 16, 64]),
                        ALU.mult, ["Sst", "etot"], ["Sst"])
                for hf in range(2):
                    self.TT("dve", Sst[:, hf * 8:(hf + 1) * 8, :], Sst[:, hf * 8:(hf + 1) * 8, :],
                            g.ps[bp + hf][:].rearrange("p (a b) -> p a b", a=8), ALU.add, ["Sst", "ps%d" % (bp + hf)], ["Sst"])

            def state_update(c, d, i):
                local_mm(c, d, i, 6)
                state_apply(c, d, 6)

            self.MEMSET("pool", Sst[:], 0.0, ["Sst"])
            order_b = [1, 0] + list(range(NCH - 1, 1, -1))

            load_chunk(order_b[0], 0)
            local_mm(order_b[0], 1, 0, 4)
            for n_, c in enumerate(order_b):
                i = n_ % 2
                nxt = n_ + 1 < NCH - 1
                if nxt:
                    load_a(order_b[n_ + 1], (n_ + 1) % 2)
                self.ACT(Sbf[i][:], Sst[:].rearrange("p a b -> p (a b)"), AF.Copy, ["Sst"], ["Sbf%d" % i])
                self.DMA("sp", g.sbwd[c], Sbf[i][:], ["Sbf%d" % i], [("sbwd", c)])
                if n_ == NCH - 1:
                    break
                state_apply(c, 1, 4 + 2 * (n_ % 2))
                if nxt:
                    load_b(order_b[n_ + 1], (n_ + 1) % 2)
                    local_mm(order_b[n_ + 1], 1, (n_ + 1) % 2, 4 + 2 * ((n_ + 1) % 2))
            self.S.emit("ssd.4")
            if stop_after == "ssd_bwd" + tag:
                self.DMA("sp", g.dbg, g.sbwd, [], [])
                self.S.emit("ssd.5")
                return True

            self.MEMSET("pool", Sst[:], 0.0, ["Sst"])
            nseg = 0
            for c in range(NCH):
                i = c % 2
                xck, xsk, btk = "xc%d" % i, "xs_tok%d" % i, "B_tok%d" % i
                yk = "yacc%d" % i
                self.DMA("sp", Sbw[i][:], g.sbwd[c], [], ["Sbw%d" % i])
                load_chunk(c, i)
                self.ACT(Sbf[i][:], Sst[:].rearrange("p a b -> p (a b)"), AF.Copy, ["Sst"], ["Sbf%d" % i])
                for gq in range(4):
                    self.MM(g.ps[2][:, gq * 128:(gq + 1) * 128], xc[i][:, 8 + gq, :], xc[i][:, 12 + gq, :], True, True,
                            [xck], ["ps2"])
                pG = g.ps[2][:].rearrange("p (a b) -> p a b", a=4)
                self.TT("dve", Gmf[i][:], pG, g.Lf.unsqueeze(1).broadcast_to([128, 4, 128]), ALU.mult, ["ps2", "consts"], ["Gmf%d" % i])
                self.TT("dve", Gmb[i][:], pG, g.Ub.unsqueeze(1).broadcast_to([128, 4, 128]), ALU.mult, ["ps2", "consts"], ["Gmb%d" % i])
                self.TT("pool", yacc[i][:], xs_tok[i][:], rowb[:, 64:80].unsqueeze(2).broadcast_to([128, 16, 64]), ALU.mult,
                        [xsk, "rowb"], [yk])
                for d in range(2):
                    Sp, Spk = (Sbf[i], "Sbf%d" % i) if d == 0 else (Sbw[i], "Sbw%d" % i)
                    for gq in range(4):
                        bk = 6 + gq // 2
                        self.MM(g.ps[bk][:, (gq % 2) * 256:(gq % 2 + 1) * 256], xc[i][:, 12 + gq, :],
                                Sp[:, gq * 256:(gq + 1) * 256], True, True, [xck, Spk], ["ps%d" % bk])
                    for hf in range(2):
                        self.TT("dve", ytmp[d][:, hf * 8:(hf + 1) * 8, :], g.ps[6 + hf][:].rearrange("p (a b) -> p a b", a=8),
                                ecs[:, c, d * 16 + hf * 8:d * 16 + hf * 8 + 8].unsqueeze(2).broadcast_to([128, 8, 64]),
                                ALU.mult, ["ps%d" % (6 + hf), "ecs"], [("ytmp", d, hf)])
                    self.TT("pool", yacc[i][:], yacc[i][:], ytmp[d][:], ALU.add, [yk, ("ytmp", d, 0), ("ytmp", d, 1)], [yk])
                    self.TT("pool", xdt[d][:], xs_tok[i][:], dt[:, c, d * 16:(d + 1) * 16].unsqueeze(2).broadcast_to([128, 16, 64]),
                            ALU.mult, [xsk, "dt"], ["xdt%d" % d])
                items = [(d, gq) for d in range(2) for gq in range(4)]
                SB = (1, 3)

                def seg_issue(n):
                    d, gq = items[n]
                    r = n % 2
                    triR = g.Lf if d == 0 else g.Ub
                    triL = g.SL if d == 0 else g.SU
                    for hh in range(4):
                        hcol = d * 16 + gq * 4 + hh
                        self.ACT(rseg[r][:, hh, :], triR, AF.Copy, ["consts", "da"], [("rseg", r, hh)], scale=da[:, c, hcol:hcol + 1])
                    self.MM(g.ps[SB[r]][:], triL, rseg[r][:].rearrange("p a b -> p (a b)"), True, True,
                            [("rseg", r, hh) for hh in range(4)] + ["consts"], ["ps%d" % SB[r]])
                seg_issue(0)
                for n, (d, gq) in enumerate(items):
                    r = n % 2
                    if n + 1 < len(items):
                        seg_issue(n + 1)
                    Gmd, Gk = (Gmf[i], "Gmf%d" % i) if d == 0 else (Gmb[i], "Gmb%d" % i)
                    self.ACT(Eb[r][:].rearrange("p a b -> p (a b)"), g.ps[SB[r]][:], AF.Exp, ["ps%d" % SB[r]], ["Eb%d" % r])
                    self.TT("dve", Mt[r][:], Eb[r][:], Gmd[:, gq, :].unsqueeze(1).broadcast_to([128, 4, 128]), ALU.mult,
                            ["Eb%d" % r, Gk], ["Mt%d" % r])
                    for hh in range(4):
                        h = gq * 4 + hh
                        bk = 4 + h // 8
                        self.MM(g.ps[bk][:, (h % 8) * 64:(h % 8 + 1) * 64], Mt[r][:, hh, :], xdt[d][:, h, :], True, True,
                                ["Mt%d" % r, "xdt%d" % d], ["ps%d" % bk])
                    if gq == 3:
                        for hf in range(2):
                            self.TT("dve", yacc[i][:, hf * 8:(hf + 1) * 8, :], yacc[i][:, hf * 8:(hf + 1) * 8, :],
                                    g.ps[4 + hf][:].rearrange("p (a b) -> p a b", a=8), ALU.add, [yk, "ps%d" % (4 + hf)], [yk])
                if c < NCH - 1:
                    state_update(c, 0, i)
                self.ACT(yzn[i][:], yacc[i][:].rearrange("p a b -> p (a b)"), AF.Copy, [yk], ["yzn%d" % i])
                pxo = g.ps[2][:].bitcast(BF16)
                for k in range(8):
                    self.TR(pxo[:, k * 128:(k + 1) * 128], yzn[i][:, k * 128:(k + 1) * 128], g.identb[:], ["yzn%d" % i, "identb"], ["ps2"])
                self.CP("dve", g.yX[:, :, c * 128:(c + 1) * 128], pxo.rearrange("p (a b) -> p a b", a=8), ["ps2"], [("yX", c)])
            self.S.emit("ssd.6")

    def phase_ssd_post(self, l, b):
        g = self.g
        w_in = g.w["w_in"][l]
        with ExitStack() as es:
            wz = g.sb(es, "wz", [128, 8, D], BF16)
            sz = [g.sb(es, "sz%d" % i, [128, 512], F32) for i in range(2)]
            gt = [g.sb(es, "gt%d" % i, [128, 8, 512], F32) for i in range(2)]
            sq = [g.sb(es, "sqp%d" % i, [128, 512], BF16) for i in range(4)]
            rs = [g.sb(es, "rsp%d" % i, [128, 512], F32) for i in range(2)]
            self.wslab(wz[:], w_in[:, O_Z:O_Z + D], "wz")
            n2 = 0
            o = g.coff[("ssdn", l)]

            def fin(ti):
                t0, t1 = TILES[ti]
                w = t1 - t0
                i = ti % 2
                for oc in range(8):
                    self.STT(g.yX[:, oc, t0:t1], gt[i][:, oc, :w], g.cols[:, o + oc:o + oc + 1], rs[i][:, :w], ALU.mult, ALU.mult,
                             [("gt", i, oc), "rsp%d" % i, "cols"], [("yXp", oc, ti)])
            nq = [0]
            for ti, (t0, t1) in enumerate(TILES):
                w = t1 - t0
                i = ti % 2
                pq = g.ps[4 + i]
                pend = []

                def ones_mm(oc, qi_):
                    self.MM(pq[:, :w], g.onesb[:], sq[qi_][:, :w], oc == 0, oc == 7, ["sqp%d" % qi_], ["ps%d" % (4 + i)])
                for oc in range(8):
                    j = n2 % 4
                    n2 += 1
                    for k in range(8):
                        self.MM(g.ps[j][:, :w], wz[:, k, oc * 128:(oc + 1) * 128], g.aT[:, k, t0:t1], k == 0, k == 7, ["wz"], ["ps%d" % j])
                    if len(pend) >= 2:
                        ones_mm(*pend.pop(0))
                    qi_ = nq[0] % 4
                    nq[0] += 1
                    self.ACT(sz[j % 2][:, :w], g.ps[j][:, :w], AF.Silu, ["ps%d" % j], ["sz%d" % (j % 2)])
                    self.TT("dve", gt[i][:, oc, :w], sz[j % 2][:, :w], g.yX[:, oc, t0:t1], ALU.mult, ["sz%d" % (j % 2)], [("gt", i, oc)])
                    self.ACT(sq[qi_][:, :w], gt[i][:, oc, :w], AF.Square, [("gt", i, oc)], ["sqp%d" % qi_])
                    pend.append((oc, qi_))
                while pend:
                    ones_mm(*pend.pop(0))
                self.ACT(rs[i][:, :w], pq[:, :w], AF.Sqrt, ["ps%d" % (4 + i)], ["rsp%d" % i], bias=EPS, scale=1.0 / D)
                self.RECIP(rs[i][:, :w], rs[i][:, :w], ["rsp%d" % i], ["rsp%d" % i])
                if ti >= 1:
                    fin(ti - 1)
            fin(len(TILES) - 1)
            self.S.emit("ssd_post.1")

    def phase_final(self):
        g = self.g
        with ExitStack() as es:
            ht = [g.sb(es, "ht%d" % i, [128, 8, 512], F32) for i in range(2)]
            sq = [g.sb(es, "sq%d" % i, [128, 8, 512], BF16) for i in range(2)]
            rs = [g.sb(es, "rs%d" % i, [128, 512], F32) for i in range(2)]
            tmp = [g.sb(es, "tmp%d" % i, [128, 8, 512], F32) for i in range(2)]
            ot = [g.sb(es, "ot%d" % i, [128, D], F32) for i in range(2)]
            n2 = 0
            n3 = 0
            for b in range(g.nb):
                for ti, (t0, t1) in enumerate(TILES[1:]):
                    w = t1 - t0
                    i = n2 % 2
                    n2 += 1
                    self.DMA("sp", ht[i][:], g.hs[b][:, :, t0:t1], [], ["ht%d" % i])
                    self.ACT(sq[i][:], ht[i][:], AF.Square, ["ht%d" % i], ["sq%d" % i])
                    for k in range(8):
                        self.MM(g.ps[i][:], g.onesb[:], sq[i][:, k, :], k == 0, k == 7, ["sq%d" % i], ["ps%d" % i])
                    self.ACT(rs[i][:], g.ps[i][:], AF.Sqrt, ["ps%d" % i], ["rs%d" % i], bias=EPS, scale=1.0 / D)
                    self.RECIP(rs[i][:], rs[i][:], ["rs%d" % i], ["rs%d" % i])
                    self.TT("dve", tmp[i][:], ht[i][:], rs[i][:].unsqueeze(1).broadcast_to([128, 8, 512]),
                            ALU.mult, ["ht%d" % i, "rs%d" % i], ["tmp%d" % i])
                    for k in range(8):
                        o = g.coff["fn"] + k
                        self.ACT(tmp[i][:, k, :], tmp[i][:, k, :], AF.Identity, ["tmp%d" % i], ["tmp%d" % i],
                                 scale=g.cols[:, o:o + 1])
                    for sub in range(4):
                        j = n3 % 2
                        n3 += 1
                        for k in range(8):
                            bk = 2 + j * 2 + k // 4
                            self.TR(g.ps[bk][:, (k % 4) * 128:(k % 4 + 1) * 128], tmp[i][:, k, sub * 128:(sub + 1) * 128],
                                    g.identf, ["tmp%d" % i, "consts"], ["ps%d" % bk])
                        self.ACT(ot[j][:, 0:512], g.ps[2 + j * 2][:], AF.Copy, ["ps%d" % (2 + j * 2)], [("ot", j, 0)])
                        self.CP("dve", ot[j][:, 512:1024], g.ps[3 + j * 2][:], ["ps%d" % (3 + j * 2)], [("ot", j, 1)])
                        tok0 = t0 - NCTX + sub * 128
                        self.DMA("pool", g.out[b, tok0:tok0 + 128, :], ot[j][:], [("ot", j, 0), ("ot", j, 1)], [])
            self.S.emit("final.1")


def host_consts():
    idx = np.arange(128)
    r, c = idx[:, None], idx[None, :]
    mats = [np.eye(128), np.ones((128, 128)), (r <= c), (r >= c), (r > c), (r < c)]
    consts = np.concatenate([m.astype(np.float32) for m in mats], axis=1)
    rows = NLAT // 64
    row = np.repeat(np.arange(rows), 64).astype(np.float32)
    colp = np.tile(np.arange(64), rows).astype(np.float32)
    half = 32
    inv = (1.0 / (np.float32(10000.0) ** (np.arange(0, half, 2, dtype=np.float32) / np.float32(half)))).astype(np.float32)
    ar = row[:, None] * inv
    ac = colp[:, None] * inv
    ang = np.concatenate([ar, ar, ac, ac], axis=-1).astype(np.float32)
    cosT = np.ones((64, T), np.float32)
    sinT = np.zeros((64, T), np.float32)
    cosT[:, NCTX:] = np.cos(ang).T
    sinT[:, NCTX:] = np.sin(ang).T
    rope = np.stack([cosT, sinT]).astype(np.float32)
    rot = np.zeros((64, 64), np.float32)
    for m in range(64):
        blk = m // 16
        if blk % 2 == 0:
            rot[m + 16, m] = -1.0
        else:
            rot[m - 16, m] = 1.0
    invc = np.zeros((4, T), np.float32)
    for gi, win in enumerate((2, 4, 8, 16)):
        left = win // 2
        for (s0, n) in ((0, NCTX), (NCTX, NLAT)):
            t = np.arange(n)
            lo = np.clip(t - left, 0, n)
            hi = np.clip(t - left + win, 0, n)
            invc[gi, s0:s0 + n] = 1.0 / (hi - lo).astype(np.float32)
    return consts, rope, rot, invc


def host_cols(inp, depth):
    off, ncols = cols_layout(depth)
    cols = np.zeros((128, ncols), np.float32)

    def fm(v):
        return np.ascontiguousarray(np.asarray(v, np.float32).reshape(-1, 128).T)
    for l in range(depth):
        cols[:, off[("gam0", l)]:off[("gam0", l)] + 8] = fm(inp["ffn1_norm"][l])
        cols[:, off[("gam1", l)]:off[("gam1", l)] + 8] = fm(inp["mix_norm"][l])
        cols[:, off[("gam2", l)]:off[("gam2", l)] + 8] = fm(inp["ffn2_norm"][l])
        cols[:, off[("bada", l)]:off[("bada", l)] + 72] = fm(inp["b_ada"][l])
        scw = np.asarray(inp["ssd_conv_w"][l], np.float32)
        cols[:, off[("scw", l)]:off[("scw", l)] + 64] = scw.T.reshape(16, 128, 4).transpose(1, 0, 2).reshape(128, 64)
        cols[:, off[("scb", l)]:off[("scb", l)] + 16] = fm(inp["ssd_conv_b"][l])
        cols[:, off[("ssdn", l)]:off[("ssdn", l)] + 8] = fm(inp["ssd_norm"][l])
        cols[:, off[("qn", l)]:off[("qn", l)] + 3] = fm(inp["mla_q_norm"][l])
        cols[:, off[("kvn", l)]:off[("kvn", l)] + 2] = fm(inp["mla_kv_norm"][l])
        cols[:, off[("psc", l)]:off[("psc", l)] + 8] = fm(inp["pool_scale"][l])
        cvw = np.asarray(inp["sconv_w"][l], np.float32)
        cols[:, off[("cvw", l)]:off[("cvw", l)] + 24] = cvw.T.reshape(8, 128, 3).transpose(1, 0, 2).reshape(128, 24)
    cols[:, off["fn"]:off["fn"] + 8] = fm(inp["final_norm"])
    return cols


def host_rows(inp, depth):
    rows = np.zeros((depth, 3, 32), np.float32)
    for l in range(depth):
        rows[l, 0] = np.asarray(inp["ssd_dt_bias"][l], np.float32).reshape(32)
        rows[l, 1] = np.asarray(inp["ssd_a_log"][l], np.float32).reshape(32)
        rows[l, 2, :16] = np.asarray(inp["ssd_d"][l], np.float32)
    return rows


_W_KEYS = ["ffn1_w_gate", "ffn1_w_up", "ffn1_w_down", "ffn2_w_gate", "ffn2_w_up", "ffn2_w_down", "w_in", "ssd_w_out",
           "mla_w_uq", "mla_w_uk", "mla_w_uv", "mla_w_out", "pool_w", "pool_w_out", "sconv_w_out", "w_o"]


def make_in_maps(inp, depth, nb, n_cores):
    consts, rope, rot, invc = host_consts()
    cols = host_cols(inp, depth)
    rows = host_rows(inp, depth)
    shared = {k: np.ascontiguousarray(np.asarray(inp[k], np.float32)[:depth]) for k in _W_KEYS}
    shared["w_ada"] = np.ascontiguousarray(np.asarray(inp["w_ada"], np.float32)[:depth])
    shared.update({"cols": cols, "rows": rows, "consts": consts, "rope": rope, "rotm": rot, "invcnt": invc})
    x = np.asarray(inp["x"], np.float32)
    ctx = np.asarray(inp["ctx"], np.float32)
    c = np.asarray(inp["c"], np.float32)
    c_ctx = np.asarray(inp["c_ctx"], np.float32)
    maps = []
    for core in range(n_cores):
        b0 = core * nb
        cc = np.stack([c[b0 + (i % nb)] for i in range(2)] + [c_ctx], axis=0)
        cT = np.ascontiguousarray(cc.T.reshape(8, 128, 3).transpose(1, 0, 2))
        m = dict(shared)
        m["x"] = np.ascontiguousarray(x[b0:b0 + nb])
        m["ctx"] = np.ascontiguousarray(ctx[b0:b0 + nb])
        m["cT"] = cT
        maps.append(m)
    return maps


_CACHE = {}


def kernel(**inputs):
    n_cores, nb, depth = 8, 2, 4
    if "nc" not in _CACHE:
        _CACHE["nc"] = build(depth, nb)[0]
    nc = _CACHE["nc"]
    maps = make_in_maps(inputs, depth, nb, n_cores)
    res = run_bass_kernel_spmd(nc, maps, core_ids=list(range(n_cores)))
    out = np.concatenate([np.asarray(r["out"], np.float32) for r in res.results], axis=0)
    return out
```
